# Optimizing a Trainium2 kernel written in Bass

```python
import jax, jax.numpy as jnp
from jax import lax
import numpy as np

D_MODEL = 1024
BATCH = 16
SEQ = 2048
DEPTH = 4
DEC_BATCH = 8
DEC_SEQ = 4096
PAST_LEN = 128

GRID_W = 64
ROPE_THETA = 10000.0
NORM_EPS = 1e-6
NEG_INF = -1e30

MLA_HEADS = 8
MLA_NOPE = 64
MLA_ROPE = 32
MLA_V = 64
MLA_DQK = MLA_NOPE + MLA_ROPE
MLA_Q_RANK = 384
MLA_KV_RANK = 256
MLA_QBLK = 128
MLA_WIDTH = MLA_HEADS * MLA_V

DIL_PAIRS = ((128, 1), (512, 4), (2048, 16))
DIL_GROUPS = 3
DIL_HPG = 4
DIL_HD = 64
DIL_HEADS = DIL_GROUPS * DIL_HPG
DIL_WIDTH = DIL_HPG * DIL_HD

NA_HEADS = 8
NA_HD = 64
NA_KH = 8
NA_KW = 16
NA_WIDTH = NA_HEADS * NA_HD

N_BRANCH = 3
IN_SIZES = (MLA_Q_RANK, MLA_KV_RANK, MLA_ROPE, MLA_WIDTH, 3 * DIL_HEADS * DIL_HD, DIL_WIDTH, 3 * NA_HEADS * NA_HD, NA_WIDTH, N_BRANCH * D_MODEL)
D_IN = 384 + 256 + 32 + 512 + 2304 + 256 + 1536 + 512 + 3072

kernel_name = 'hybrid_mla_dilated_natten_encoder'


def rmsnorm(x, g):
    x32 = x.astype(jnp.float32)
    y = x32 * lax.rsqrt(jnp.mean(x32 * x32, axis=-1, keepdims=True) + NORM_EPS)
    return (y * g.astype(jnp.float32)).astype(x.dtype)


def rope(x, pos):
    half = x.shape[-1] // 2
    inv = ROPE_THETA ** (-jnp.arange(half, dtype=jnp.float32) * 2.0 / x.shape[-1])
    ang = pos[:, None] * inv[None, :]
    cos = jnp.cos(ang)[:, None, :].astype(x.dtype)
    sin = jnp.sin(ang)[:, None, :].astype(x.dtype)
    x1, x2 = x[..., :half], x[..., half:]
    return jnp.concatenate([x1 * cos - x2 * sin, x2 * cos + x1 * sin], axis=-1)


def mla_attention(cq, ckv, kr, g_q, w_uq, g_kv, w_ukv):
    b, s, _ = cq.shape
    pos = jnp.arange(s, dtype=jnp.float32)
    q = (rmsnorm(cq, g_q) @ w_uq).reshape(b, s, MLA_HEADS, MLA_DQK)
    q = jnp.concatenate([q[..., :MLA_NOPE], rope(q[..., MLA_NOPE:], pos)], axis=-1)
    kv = (rmsnorm(ckv, g_kv) @ w_ukv).reshape(b, s, MLA_HEADS, MLA_NOPE + MLA_V)
    k_rope = rope(kr[:, :, None, :], pos)
    k = jnp.concatenate([kv[..., :MLA_NOPE], jnp.broadcast_to(k_rope, (b, s, MLA_HEADS, MLA_ROPE))], axis=-1)
    v = kv[..., MLA_NOPE:]
    scale = MLA_DQK ** -0.5
    qb = q.reshape(b, s // MLA_QBLK, MLA_QBLK, MLA_HEADS, MLA_DQK).transpose(1, 0, 2, 3, 4)

    def block(qi):
        sc = jnp.einsum('bqhd,bkhd->bhqk', qi, k).astype(jnp.float32) * scale
        p = jax.nn.softmax(sc, axis=-1).astype(v.dtype)
        return jnp.einsum('bhqk,bkhd->bqhd', p, v)

    o = lax.map(block, qb)
    return o.transpose(1, 0, 2, 3, 4).reshape(b, s, MLA_WIDTH)


def band_attention(q, k, v, n):
    b, g, L, h, hd = q.shape
    nb = -(-L // n)
    lp = nb * n
    q = jnp.pad(q, ((0, 0), (0, 0), (0, lp - L), (0, 0), (0, 0)))
    pad_kv = ((0, 0), (0, 0), (n, lp - L + n), (0, 0), (0, 0))
    k = jnp.pad(k, pad_kv)
    v = jnp.pad(v, pad_kv)

    def windows(x):
        return jnp.concatenate([x[:, :, j * n:j * n + lp].reshape(b, g, nb, n, h, hd) for j in range(3)], axis=3)

    kw, vw = windows(k), windows(v)
    qb = q.reshape(b, g, nb, n, h, hd)
    sc = jnp.einsum('bgiqhd,bgikhd->bghiqk', qb, kw).astype(jnp.float32) * (hd ** -0.5)
    qpos = jnp.arange(lp).reshape(nb, n)[:, :, None]
    kpos = (jnp.arange(nb) * n - n)[:, None, None] + jnp.arange(3 * n)[None, None, :]
    mask = (jnp.abs(kpos - qpos) <= n) & (kpos >= 0) & (kpos < L)
    sc = jnp.where(mask, sc, NEG_INF)
    lse = jax.nn.logsumexp(sc, axis=-1)
    p = jnp.exp(sc - lse[..., None]).astype(v.dtype)
    o = jnp.einsum('bghiqk,bgikhd->bgiqhd', p, vw).reshape(b, g, lp, h, hd)[:, :, :L]
    lse = lse.reshape(b, g, h, lp).transpose(0, 1, 3, 2)[:, :, :L]
    return o, lse


def dilated_group(q, k, v, d, n):
    b, s, h, hd = q.shape
    L = s // d
    to_cls = lambda x: x.reshape(b, L, d, h, hd).transpose(0, 2, 1, 3, 4)
    o, lse = band_attention(to_cls(q), to_cls(k), to_cls(v), n)
    return o.transpose(0, 2, 1, 3, 4).reshape(b, s, h, hd), lse.transpose(0, 2, 1, 3).reshape(b, s, h)


def dilated_attention(q, k, v):
    b, s = q.shape[0], q.shape[1]
    outs, lses = [], []
    for gi, (w, d) in enumerate(DIL_PAIRS):
        sl = slice(gi * DIL_HPG, (gi + 1) * DIL_HPG)
        o, l = dilated_group(q[:, :, sl], k[:, :, sl], v[:, :, sl], d, w // (2 * d))
        outs.append(o)
        lses.append(l)
    wts = jax.nn.softmax(jnp.stack(lses), axis=0)
    o = jnp.sum(wts[..., None].astype(q.dtype) * jnp.stack(outs), axis=0)
    return o.reshape(b, s, DIL_WIDTH)


def neighbourhood_attention(q, k, v, rpb):
    b, s, h, hd = q.shape
    rows = s // GRID_W
    kh = min(NA_KH, rows)
    qg = q.reshape(b, rows, GRID_W, h, hd)
    r = jnp.arange(rows)
    rs = jnp.clip(r - kh // 2, 0, rows - kh)
    row_idx = rs[:, None] + jnp.arange(kh)[None, :]
    kg = k.reshape(b, rows, GRID_W, h, hd)[:, row_idx]
    vg = v.reshape(b, rows, GRID_W, h, hd)[:, row_idx]
    sc = jnp.einsum('brchd,brjwhd->bhrcjw', qg, kg).astype(jnp.float32) * (hd ** -0.5)
    col = jnp.arange(GRID_W)
    cs = jnp.clip(col - NA_KW // 2, 0, GRID_W - NA_KW)
    colmask = (col[None, :] >= cs[:, None]) & (col[None, :] < cs[:, None] + NA_KW)
    roff = row_idx - r[:, None] + NA_KH - 1
    coff = jnp.clip(col[None, :] - col[:, None] + NA_KW - 1, 0, 2 * NA_KW - 2)
    bias = rpb[:, roff[:, :, None, None], coff[None, None, :, :]].transpose(0, 1, 3, 2, 4)
    sc = jnp.where(colmask[:, None, :], sc + bias[None].astype(jnp.float32), NEG_INF)
    p = jax.nn.softmax(sc.reshape(b, h, rows, GRID_W, kh * GRID_W), axis=-1)
    p = p.reshape(b, h, rows, GRID_W, kh, GRID_W).astype(v.dtype)
    o = jnp.einsum('bhrcjw,brjwhd->brchd', p, vg)
    return o.reshape(b, s, NA_WIDTH)


def encoder_layer(x, c_act, w_ada, b_ada, g_pre, g_post, w_in, g_q, w_uq, g_kv, w_ukv, rpb, w_pa, w_pb, w_pc, w_out):
    b, s, _ = x.shape
    shift, scale, gate = jnp.split(c_act @ w_ada + b_ada, 3, axis=-1)
    h = rmsnorm(x, g_pre) * (1 + scale[:, None, :]) + shift[:, None, :]
    z = h @ w_in
    cuts = [int(c) for c in np.cumsum(IN_SIZES)[:-1]]
    cq, ckv, kr, gate_a, qkv_b, gate_b, qkv_c, gate_c, merge = jnp.split(z, cuts, axis=-1)
    o_a = mla_attention(cq, ckv, kr, g_q, w_uq, g_kv, w_ukv)
    pos = jnp.arange(s, dtype=jnp.float32)
    qkv_b = qkv_b.reshape(b, s, 3, DIL_HEADS, DIL_HD)
    o_b = dilated_attention(rope(qkv_b[:, :, 0], pos), rope(qkv_b[:, :, 1], pos), qkv_b[:, :, 2])
    qkv_c = qkv_c.reshape(b, s, 3, NA_HEADS, NA_HD)
    o_c = neighbourhood_attention(qkv_c[:, :, 0], qkv_c[:, :, 1], qkv_c[:, :, 2], rpb)
    p_a = (o_a * jax.nn.silu(gate_a)) @ w_pa
    p_b = (o_b * jax.nn.silu(gate_b)) @ w_pb
    p_c = (o_c * jax.nn.silu(gate_c)) @ w_pc
    mg = jax.nn.sigmoid(merge.reshape(b, s, N_BRANCH, D_MODEL))
    mixed = mg[:, :, 0] * p_a + mg[:, :, 1] * p_b + mg[:, :, 2] * p_c
    out = mixed @ w_out
    return x + gate[:, None, :] * rmsnorm(out, g_post)


def trunk(x, c, w_ada, b_ada, g_pre, g_post, w_in, g_q, w_uq, g_kv, w_ukv, rpb, w_pa, w_pb, w_pc, w_out):
    c_act = jax.nn.silu(c)
    for l in range(DEPTH):
        x = encoder_layer(x, c_act, w_ada[l], b_ada[l], g_pre[l], g_post[l], w_in[l], g_q[l], w_uq[l],
                          g_kv[l], w_ukv[l], rpb[l], w_pa[l], w_pb[l], w_pc[l], w_out[l])
    return x


def setup_inputs(seed: int = 0) -> dict:
    key = jax.random.key(seed)
    ks = jax.random.split(key, 18)
    f32 = jnp.float32
    nrm = lambda k, shape, sc: jax.random.normal(k, shape, f32) * sc
    return {
        'x_prompt': nrm(ks[0], (BATCH, SEQ, D_MODEL), 1.0),
        'x_sample': nrm(ks[1], (DEC_BATCH, DEC_SEQ, D_MODEL), 1.0),
        'c_prompt': nrm(ks[2], (BATCH, D_MODEL), 1.0),
        'c_sample': nrm(ks[3], (DEC_BATCH, D_MODEL), 1.0),
        'w_ada': nrm(ks[4], (DEPTH, D_MODEL, 3 * D_MODEL), 0.5 * D_MODEL ** -0.5),
        'b_ada': nrm(ks[5], (DEPTH, 3 * D_MODEL), 0.01),
        'g_pre': 1.0 + nrm(ks[6], (DEPTH, D_MODEL), 0.02),
        'g_post': 1.0 + nrm(ks[7], (DEPTH, D_MODEL), 0.02),
        'w_in': nrm(ks[8], (DEPTH, D_MODEL, D_IN), D_MODEL ** -0.5),
        'g_q': 1.0 + nrm(ks[9], (DEPTH, MLA_Q_RANK), 0.02),
        'w_uq': nrm(ks[10], (DEPTH, MLA_Q_RANK, MLA_HEADS * MLA_DQK), MLA_Q_RANK ** -0.5),
        'g_kv': 1.0 + nrm(ks[11], (DEPTH, MLA_KV_RANK), 0.02),
        'w_ukv': nrm(ks[12], (DEPTH, MLA_KV_RANK, MLA_HEADS * (MLA_NOPE + MLA_V)), MLA_KV_RANK ** -0.5),
        'rpb': nrm(ks[13], (DEPTH, NA_HEADS, 2 * NA_KH - 1, 2 * NA_KW - 1), 0.1),
        'w_pa': nrm(ks[14], (DEPTH, MLA_WIDTH, D_MODEL), MLA_WIDTH ** -0.5),
        'w_pb': nrm(ks[15], (DEPTH, DIL_WIDTH, D_MODEL), DIL_WIDTH ** -0.5),
        'w_pc': nrm(ks[16], (DEPTH, NA_WIDTH, D_MODEL), NA_WIDTH ** -0.5),
        'w_out': nrm(ks[17], (DEPTH, D_MODEL, D_MODEL), D_MODEL ** -0.5),
    }


def reference(x_prompt, x_sample, c_prompt, c_sample, w_ada, b_ada, g_pre, g_post, w_in, g_q, w_uq, g_kv, w_ukv, rpb, w_pa, w_pb, w_pc, w_out):
    y_prompt = trunk(x_prompt, c_prompt, w_ada, b_ada, g_pre, g_post, w_in, g_q, w_uq, g_kv, w_ukv, rpb, w_pa, w_pb, w_pc, w_out)
    y_sample = trunk(x_sample, c_sample, w_ada, b_ada, g_pre, g_post, w_in, g_q, w_uq, g_kv, w_ukv, rpb, w_pa, w_pb, w_pc, w_out)
    return (y_prompt, y_sample)
```

```python
import contextlib
import numpy as np
import ml_dtypes
import concourse.bass as bass
import concourse.mybir as mybir
from concourse.bass_utils import run_bass_kernel_spmd

F32 = mybir.dt.float32
BF16 = mybir.dt.bfloat16
AF = mybir.ActivationFunctionType
ALU = mybir.AluOpType
NEG = -1e30
EPS = 1e-6
D = 1024
DIN = 8864
C_GA, C_QKVB, C_GB, C_QKVC, C_GC, C_MG = 672, 1184, 3488, 3744, 5280, 5792


class Sched:
    ENGS = ('pe', 'act', 'dve', 'pool', 'sp')

    def __init__(self):
        self.ops = []
        self.last_w = {}
        self.readers = {}

    def add(self, eng, fn, r=(), w=(), dma=0, semkey=None):
        deps = set()
        ops = self.ops
        for k in r:
            lw = self.last_w.get(k)
            if lw is not None:
                deps.add(lw)
        for k in w:
            lw = self.last_w.get(k)
            if lw is not None:
                deps.add(lw)
            for x in self.readers.get(k, ()):
                deps.add(x)
        oid = len(ops)
        ops.append([eng, fn, deps, dma, semkey])
        for k in r:
            lst = self.readers.setdefault(k, [])
            if not dma:
                lst[:] = [x for x in lst if ops[x][3] or ops[x][0] != eng]
            lst.append(oid)
        for k in w:
            self.last_w[k] = oid
            self.readers[k] = []
        return oid

    def emit(self, nc, semctx):
        ops = self.ops
        n = len(ops)
        needed = [False] * n
        for o in ops:
            for d in o[2]:
                od = ops[d]
                if (not od[3]) and od[0] == 'pe' and o[0] == 'pe' and not o[3]:
                    continue
                needed[d] = True
        sig = [None] * n
        cnt = {}
        for i, o in enumerate(ops):
            if o[3]:
                key = ('dma', o[0], o[4])
                cnt[key] = cnt.get(key, 0) + 16 * o[3]
                sig[i] = (key, cnt[key])
            elif needed[i]:
                key = ('eng', o[0])
                cnt[key] = cnt.get(key, 0) + 1
                sig[i] = (key, cnt[key])
        sems = {key: semctx("s%d" % j) for j, key in enumerate(cnt)}
        streams = {e: [] for e in self.ENGS}
        for i, o in enumerate(ops):
            streams[o[0]].append(i)

        def run_stream(ename, eo):
            waited = {}
            for i in streams[ename]:
                o = ops[i]
                need = {}
                for d in o[2]:
                    od = ops[d]
                    if (not od[3]) and od[0] == 'pe' and ename == 'pe' and not o[3]:
                        continue
                    k, v = sig[d]
                    if need.get(k, 0) < v:
                        need[k] = v
                for k, v in need.items():
                    if waited.get(k, 0) < v:
                        eo.wait_ge(sems[k], v)
                        waited[k] = v
                res = o[1](eo)
                if o[3]:
                    if not isinstance(res, (list, tuple)):
                        res = [res]
                    assert len(res) == o[3]
                    for ins in res:
                        ins.then_inc(sems[sig[i][0]], 16)
                elif needed[i]:
                    res.then_inc(sems[sig[i][0]], 1)
        return run_stream, len(sems)


class Rot:
    def __init__(self, name, tiles, keys=None):
        self.name, self.tiles, self.i = name, tiles, 0
        self.keys = keys if keys is not None else [(name, j) for j in range(len(tiles))]

    def next(self):
        j = self.i % len(self.tiles)
        self.i += 1
        return self.tiles[j], self.keys[j]


def build_program(seq_lens, depth):
    nseq = len(seq_lens)
    Smax = max(seq_lens)
    Tmax = Smax // 128
    nc = bass.Bass("TRN2", target_bir_lowering=False)
    es = contextlib.ExitStack()

    def din(name, shape, dt=F32):
        return nc.dram_tensor(name, list(shape), dt, kind="ExternalInput").ap()

    def dscr(name, shape, dt):
        return nc.dram_tensor(name, list(shape), dt, kind="Internal").ap()

    xs = [din("x%d" % i, [S, D]) for i, S in enumerate(seq_lens)]
    ys = [nc.dram_tensor("y%d" % i, [S, D], F32, kind="ExternalOutput").ap() for i, S in enumerate(seq_lens)]
    c_d = din("c", [nseq, D])
    w_ada = din("w_ada", [depth, D, 3 * D]); b_ada = din("b_ada", [depth, 3 * D])
    g_pre = din("g_pre", [depth, D]); g_post = din("g_post", [depth, D])
    w_in = din("w_in", [depth, D, DIN]); w_krp = din("w_krp", [depth, D, 32])
    g_q = din("g_q", [depth, 384]); w_uqh = din("w_uqh", [depth, 384, 768]); w_uqrp = din("w_uqrp", [depth, 384, 768])
    g_kv = din("g_kv", [depth, 256]); w_ukv = din("w_ukv", [depth, 256, 1024])
    rpbf = din("rpbf", [depth, 8, 24, 127])
    w_pa = din("w_pa", [depth, 512, D]); w_pb = din("w_pb", [depth, 256, D]); w_pc = din("w_pc", [depth, 512, D])
    w_out = din("w_out", [depth, D, D])
    identf_d = din("ident_f", [128, 128]); identb_d = din("ident_b", [128, 128], BF16)
    jmat_d = din("jmat", [128, 128], BF16); onesb_d = din("ones_b", [128, 128], BF16)
    cos64_d = din("cos64", [128, Smax]); ssin64_d = din("ssin64", [128, Smax])
    cos32_d = din("cos32", [32, Smax]); ssin32_d = din("ssin32", [32, Smax])
    tm_d = din("tm", [128, 1152], BF16); cmask_d = din("cmask", [128, 2, 23 * 64], BF16)

    NCHM = Smax // 512
    hT_d = dscr("hT_d", [NCHM, 128, 8, 512], BF16); gate_d = dscr("gate_d", [NCHM, 128, 10, 512], BF16)
    og_d = dscr("og_d", [NCHM, 128, 10, 512], BF16); mix_d = dscr("mix_d", [NCHM, 128, 8, 512], BF16)
    cqn_d = dscr("cqn_d", [NCHM, 128, 3, 512], BF16); ckvn_d = dscr("ckvn_d", [NCHM, 128, 2, 512], BF16)
    kr_d = dscr("kr_d", [32, Smax], BF16); ada_d = dscr("ada_d", [depth, nseq, 3 * D], F32)

    def sb(name, shape, dt=F32):
        return es.enter_context(nc.sbuf_tensor(name, list(shape), dt))

    def ps(name, shape, dt=F32):
        return es.enter_context(nc.psum_tensor(name, list(shape), dt))

    with es:
        ident_f = sb("ident_f_s", [128, 128]); ident_b = sb("ident_b_s", [128, 128], BF16)
        jmat = sb("jmat_s", [128, 128], BF16); ones_b = sb("ones_b_s", [128, 128], BF16)
        tm = sb("tm_s", [128, 1152], BF16); cmask = sb("cmask_s", [128, 2, 23 * 64], BF16)
        wA = [sb("wA%d" % i, [128, 8, 512], BF16) for i in range(2)]
        hc = [sb("hc%d" % i, [128, 8, 512], BF16) for i in range(2)]
        KQ = [sb("KQ%d" % i, [128, Smax], BF16) for i in range(2)]
        Var = sb("Var", [128, Tmax, 192], BF16)
        VT = sb("VT", [128, Smax], BF16)
        PTb = [sb("PT%d" % i, [128, 512], BF16) for i in range(4)]
        ogt = [sb("ogt%d" % i, [128, 512], BF16) for i in range(2)]
        gtile = [sb("gt%d" % i, [128, 512], BF16) for i in range(2)]
        tmpf = [sb("tmp%d" % i, [128, 512]) for i in range(6)]
        xt = [sb("xt%d" % i, [128, D]) for i in range(2)]
        xn = [sb("xn%d" % i, [128, D]) for i in range(1)]
        G_bc = sb("G_bc", [128, D])
        cols = sb("cols", [128, 64])
        small = sb("small", [128, 16])
        numacc = sb("numacc", [128, Smax]); denacc = sb("denacc", [128, Smax])
        latc = [sb("latc%d" % i, [128, 3, 512], BF16) for i in range(2)]
        krc = [sb("krc%d" % i, [32, 512], BF16) for i in range(2)]
        Qc = [sb("Qc%d" % i, [96, 512], BF16) for i in range(2)]
        r64 = [sb("r64_%d" % i, [128, 2, 512]) for i in range(2)]
        wukv_s = sb("wukv_s", [128, 2, 1024], BF16); wuqh_s = sb("wuqh_s", [128, 3, 768], BF16)
        wuqrp_s = sb("wuqrp_s", [128, 3, 768], BF16)
        Etab = sb("Etab", [128, 4, 23 * 64], BF16)
        Xv = sb("Xv", [128, 23 * 64], BF16)
        wp_s = sb("wp_s", [128, 10, 128], BF16)
        ogc = sb("ogc", [128, 10, 512], BF16)
        Xraw = ogc[:, 0:6, :].rearrange("p a b -> p (a b)").bitcast(F32)[:, 0:23 * 64]
        cact = sb("cact", [128, 8, 4])
        bada_t = sb("bada_t", [4, 512]); adarow_t = sb("adarow_t", [4, 512])

        pj = [ps("pj%d" % i, [128, 512]) for i in range(2)]
        scp = [ps("sc%d" % i, [128, 512]) for i in range(3)]
        acp = [ps("ac%d" % i, [128, 512]) for i in range(2)]
        ssp = scp[2]
        tb = ps("tb", [128, 4, 128], BF16)

        S = Sched()
        pjrot = Rot('pj', pj); pj4rot = Rot('pj4', pj + acp, [('pj', 0), ('pj', 1), ('ac', 0), ('ac', 1)]); scrot = Rot('sc', scp); acrot = Rot('ac', acp); ptrot = Rot('pt', PTb)
        hcrot = Rot('hc', hc); xtrot = Rot('xt', xt); xnrot = Rot('xn', xn); tmprot = Rot('tmp', tmpf)
        ogrot = Rot('ogt', ogt); gtrot = Rot('gt', gtile); latcrot = Rot('latc', latc); latkrot = latcrot
        krcrot = Rot('krc', krc); qcrot = Rot('Qc', Qc); r64rot = Rot('r64', r64); r32rot = r64rot

        STATS = {}
        def _stat(eng, tag, n):
            d = STATS.setdefault((eng, tag), [0, 0])
            d[0] += 1; d[1] += n
        def _fsz(ap):
            n = 1
            for x in ap.shape[1:]:
                n *= x
            return n
        def op(eng, meth, r=(), w=(), **kw):
            o = kw.get('out', kw.get('ap'))
            _stat(eng, meth + ('_' + str(kw['func']).split('.')[-1] if 'func' in kw else ''), _fsz(o))
            S.add(eng, (lambda e: getattr(e, meth)(**kw)), r, w)

        def dma(eng, out, in_, r=(), w=(), sem=None, **kw):
            S.add(eng, (lambda e: e.dma_start(out=out, in_=in_, **kw)), r, w, dma=1, semkey=sem)

        def mm(out, lhsT, rhs, start, stop, r, w):
            _stat('pe', 'mm', _fsz(out))
            S.add('pe', (lambda e: e.matmul(out, lhsT=lhsT, rhs=rhs, start=start, stop=stop)), r, w)

        def kt_view(ap2d):
            return ap2d.rearrange("(k p) n -> p k n", p=128)

        for (dst, src, k) in [(ident_f, identf_d, 'ident_f'), (ident_b, identb_d, 'ident_b'), (jmat, jmat_d, 'jmat'),
                              (ones_b, onesb_d, 'ones_b'), (tm, tm_d, 'tm'), (cmask, cmask_d, 'cmask')]:
            dma('sp', dst[:], src, w=[k], sem=k)
        op('dve', 'memset', w=['Var_ones'], ap=Var[:, :, 64:128], constant=1.0)

        S.add('sp', (lambda e: [e.dma_start(out=cact[:, :, s_], in_=c_d[s_].rearrange("(k p) -> p k", p=128),
                                            allow_slow_non_contiguous=True) for s_ in range(nseq)]),
              r=[], w=['cact'], dma=nseq, semkey='cact')
        op('act', 'activation', r=['cact'], w=['cact'], out=cact[:, :, 0:nseq], in_=cact[:, :, 0:nseq], func=AF.Silu)
        for l in range(depth):
            for cb in range(6):
                bada_s, badk = bada_t, 'bada_t'
                adarow, adak = adarow_t, 'adarow_t'
                dma('sp', bada_s[0:nseq, :], b_ada[l, cb * 512:(cb + 1) * 512].partition_broadcast(nseq), r=[], w=[badk], sem=badk)
                for kc in range(8):
                    wt, wk = tmprot.next()
                    dma('sp', wt[:], w_ada[l, kc * 128:(kc + 1) * 128, cb * 512:(cb + 1) * 512], w=[wk], sem=wk)
                    mm(ssp[0:nseq, :], cact[:, kc, 0:nseq], wt[:], kc == 0, kc == 7, r=['cact', wk], w=[('sc', 2)])
                op('dve', 'tensor_tensor', r=[('sc', 2), badk], w=[adak], out=adarow[0:nseq, :], in0=ssp[0:nseq, :],
                   in1=bada_s[0:nseq, :], op=ALU.add)
                dma('pool', ada_d[l, :, cb * 512:(cb + 1) * 512], adarow[0:nseq, :], r=[adak], w=[('adad', l)], sem=adak)

        def load_w(slot, segs, l):
            wk = ('wA', slot)
            fns = []
            for (src, off) in segs:
                n = src.shape[1]
                fns.append((wA[slot][:, :, off:off + n], kt_view(src)))
            S.add('pool', (lambda e: [e.dma_start(out=o, in_=i) for (o, i) in fns]), r=[], w=[wk], dma=len(fns), semkey=wk)
            return wk

        def load_hc(c):
            t, k = hcrot.next()
            dma('sp', t[:], hT_d[c], r=[('hTd', c)], w=[k], sem=k)
            return t, k

        def proj(ht, hk, slot, wk, off, m):
            p, pk = pj4rot.next()
            for kc in range(8):
                mm(p[0:m, :], wA[slot][:, kc, off:off + m], ht[:, kc, :], kc == 0, kc == 7, r=[wk, hk], w=[pk])
            return p, pk

        def attend(blocks, scale, look=2):
            acc, ak = acrot.next()
            nb = {}
            for b in blocks:
                nb[b[2]] = nb.get(b[2], 0) + 1
            seen = {}
            pend = []

            def pv(item):
                (a, wdt, Vap, pt, pk) = item
                seen[a] = seen.get(a, 0) + 1
                mm(acc[:, a:a + wdt], Vap, pt[:, a:a + wdt], seen[a] == 1, seen[a] == nb[a], r=[pk, 'Var', 'Var_ones'], w=[ak])

            for (Kap, Qap, a, wdt, mask_ap, Vap, rkeys) in blocks:
                sp_, sk = scrot.next()
                mm(sp_[:, a:a + wdt], Kap, Qap, True, mask_ap is None, r=rkeys, w=[sk])
                if mask_ap is not None:
                    mm(sp_[:, a:a + wdt], ident_b[:], mask_ap, False, True, r=['ident_b', 'tm', 'Etab'], w=[sk])
                pt, pk = ptrot.next()
                op('act', 'activation', r=[sk], w=[pk], out=pt[:, a:a + wdt], in_=sp_[:, a:a + wdt], func=AF.Exp, scale=scale)
                pend.append((a, wdt, Vap, pt, pk))
                if len(pend) > look:
                    pv(pend.pop(0))
            while pend:
                pv(pend.pop(0))
            return acc, ak

        def finish_head(e, num_ap, den_ap, rk, gt, gk, og, ok, split=False):
            nr = slice(e * 64, e * 64 + 64)
            dr = slice(64 - e * 64, 128 - e * 64)
            t2, k2 = tmprot.next()
            if split:
                t1, k1 = tmprot.next()
                op('act', 'activation', r=rk, w=[k1], out=t1[nr, :], in_=den_ap[dr, :], func=AF.Ln)
                op('act', 'activation', r=[k1], w=[k2], out=t2[nr, :], in_=t1[nr, :], func=AF.Exp, scale=-1.0)
                t3, k3 = tmprot.next()
                op('pool', 'tensor_tensor', r=rk + [k2], w=[k3], out=t3[nr, :], in0=num_ap[nr, :], in1=t2[nr, :], op=ALU.mult)
                op('pool', 'tensor_tensor', r=[k3, gk], w=[ok], out=og[nr, :], in0=t3[nr, :], in1=gt[nr, :], op=ALU.mult)
                return
            if True:
                op('dve', 'reciprocal', r=rk, w=[k2], out=t2[nr, :], in_=den_ap[dr, :])
                t3, k3 = tmprot.next()
                op('dve', 'tensor_tensor', r=rk + [k2], w=[k3], out=t3[nr, :], in0=num_ap[nr, :], in1=t2[nr, :], op=ALU.mult)
            op('dve', 'tensor_tensor', r=[k3, gk], w=[ok], out=og[nr, :], in0=t3[nr, :], in1=gt[nr, :], op=ALU.mult)

        def v_transposes(nt, t_begin=0):
            for t0 in range(t_begin, nt, 4):
                for i in range(4):
                    S.add('pe', (lambda en, i=i, t0=t0: en.transpose(tb[:, i, :], VT[:, (t0 + i) * 128:(t0 + i + 1) * 128], ident_b[:])),
                          r=['VT', 'ident_b'], w=['tb'])
                op('dve', 'tensor_copy', r=['tb'], w=['Var'], out=Var[:, t0:t0 + 4, 0:64], in_=tb[:, :, 0:64])
                op('act', 'activation', r=['tb'], w=['Var'], out=Var[:, t0:t0 + 4, 128:192], in_=tb[:, :, 64:128], func=AF.Copy)

        def v_transposes_c(c):
            t0 = 4 * c
            for i in range(4):
                S.add('pe', (lambda en, i=i, t0=t0: en.transpose(tb[:, i, :], VT[:, (t0 + i) * 128:(t0 + i + 1) * 128], ident_b[:])),
                      r=[('VTc', c), 'VT', 'ident_b'], w=['tb'])
            op('dve', 'tensor_copy', r=['tb'], w=['Var'], out=Var[:, t0:t0 + 4, 0:64], in_=tb[:, :, 0:64])
            op('act', 'activation', r=['tb'], w=['Var'], out=Var[:, t0:t0 + 4, 128:192], in_=tb[:, :, 64:128], func=AF.Copy)

        def load_gate(rowtile, c):
            gt, gk = gtrot.next()
            dma('sp', gt[:], gate_d[c, :, rowtile, :], r=[('gated', rowtile, c)], w=[gk], sem=gk)
            return gt, gk

        def store_og(og, ok, rowtile, c):
            dma('pool', og_d[c, :, rowtile, :], og[:], r=[ok], w=[('ogd', rowtile, c)], sem=ok)

        def rope64_evac(p, pk, c, rt, rk, out_ap, outkeys):
            t1, k1 = tmprot.next()
            op('dve', 'tensor_tensor', r=[pk, rk], w=[k1], out=t1[:], in0=p[:], in1=rt[:, 0, :], op=ALU.mult)
            t2, k2 = tmprot.next()
            for blk in range(4):
                src = blk ^ 1
                op('dve', 'tensor_tensor', r=[pk, rk], w=[k2], out=t2[blk * 32:(blk + 1) * 32, :],
                   in0=p[src * 32:(src + 1) * 32, :], in1=rt[blk * 32:(blk + 1) * 32, 1, :], op=ALU.mult)
            op('dve', 'tensor_tensor', r=[k1, k2], w=outkeys, out=out_ap, in0=t1[:], in1=t2[:], op=ALU.add)

        import os as _os
        STOP = int(_os.environ.get('KSTOP', '99'))
        class _Stop(Exception):
            pass
        def chk(k):
            if STOP < k:
                raise _Stop()
        try:
          for si, SL in enumerate(seq_lens):
              NCH = SL // 512
              NT = SL // 128
              xsrc = xs[si]
              ydst = ys[si]
              for l in range(depth):
                  src_x = xsrc if l == 0 else ydst
                  adl = ada_d[l, si]
                  dma('sp', cols[:, 0:8], adl[0:D].rearrange("(k p) -> p k", p=128), r=[('adad', l)], w=['c_sh'], sem='c_sh', allow_slow_non_contiguous=True)
                  dma('sp', cols[:, 8:16], adl[D:2 * D].rearrange("(k p) -> p k", p=128), r=[('adad', l)], w=['c_sc'], sem='c_sc', allow_slow_non_contiguous=True)
                  dma('sp', cols[:, 16:24], g_pre[l].rearrange("(k p) -> p k", p=128), w=['c_gp'], sem='c_gp', allow_slow_non_contiguous=True)
                  dma('sp', cols[:, 32:35], g_q[l].rearrange("(k p) -> p k", p=128), w=['c_gq'], sem='c_gq', allow_slow_non_contiguous=True)
                  dma('sp', cols[:, 36:38], g_kv[l].rearrange("(k p) -> p k", p=128), w=['c_gkv'], sem='c_gkv', allow_slow_non_contiguous=True)
                  op('dve', 'scalar_tensor_tensor', r=['c_sc', 'c_gp'], w=['c_A'], out=cols[:, 24:32], in0=cols[:, 8:16], scalar=1.0,
                     in1=cols[:, 16:24], op0=ALU.add, op1=ALU.mult)
                  dma('sp', G_bc[:], adl[2 * D:3 * D].partition_broadcast(128), r=[('adad', l)], w=['G_bc'], sem='G_bc')
                  dma('sp', xn[0][:], g_post[l].partition_broadcast(128), w=[('xn', 0)], sem='gp_bc')
                  op('dve', 'tensor_tensor', r=['G_bc', ('xn', 0)], w=['G_bc'], out=G_bc[:], in0=G_bc[:], in1=xn[0][:], op=ALU.mult)
                  S.add('pool', (lambda e, l=l: [e.dma_start(out=wukv_s[:], in_=kt_view(w_ukv[l])),
                                                 e.dma_start(out=wuqh_s[:], in_=kt_view(w_uqh[l])),
                                                 e.dma_start(out=wuqrp_s[:], in_=kt_view(w_uqrp[l]))]),
                        r=[], w=['wmla'], dma=3, semkey='wmla')

                  chk(1)
                  for c in range(NCH):
                      hs, hk = hcrot.next()
                      for i in range(4):
                          t = 4 * c + i
                          xtt, xk = xtrot.next()
                          dma('sp', xtt[:], src_x[t * 128:(t + 1) * 128, :], r=[('y', si, t)], w=[xk], sem=xk)
                          xnt, xnk = xnrot.next()
                          ta, tak = tmprot.next()
                          tbb, tbk = tmprot.next()
                          op('dve', 'scalar_tensor_tensor', r=[xk], w=[tak, 'ssqa'], out=ta[:], in0=xtt[:, 0:512], scalar=1.0, in1=xtt[:, 0:512],
                             op0=ALU.mult, op1=ALU.mult, accum_out=small[:, 0:1])
                          op('dve', 'scalar_tensor_tensor', r=[xk], w=[tbk, 'ssqb'], out=tbb[:], in0=xtt[:, 512:1024], scalar=1.0, in1=xtt[:, 512:1024],
                             op0=ALU.mult, op1=ALU.mult, accum_out=small[:, 3:4])
                          op('dve', 'tensor_tensor', r=['ssqa', 'ssqb'], w=['ssq'], out=small[:, 9:10], in0=small[:, 0:1], in1=small[:, 3:4], op=ALU.add)
                          op('act', 'activation', r=['ssq'], w=['sd'], out=small[:, 1:2], in_=small[:, 9:10], func=AF.Sqrt, scale=1.0 / D, bias=EPS)
                          op('dve', 'reciprocal', r=['sd'], w=['rs'], out=small[:, 2:3], in_=small[:, 1:2])
                          op('dve', 'tensor_scalar', r=[xk, 'rs'], w=[xnk], out=xnt[:], in0=xtt[:], scalar1=small[:, 2:3], scalar2=None, op0=ALU.mult)
                          bnk = (pj, ('pj', 0), ('pj', 1)) if t % 2 == 0 else (acp, ('ac', 0), ('ac', 1))
                          for kc in range(8):
                              S.add('pe', (lambda en, kc=kc, xnt=xnt, bb=bnk[0]: en.transpose(bb[kc // 4][:, (kc % 4) * 128:(kc % 4 + 1) * 128],
                                                                                    xnt[:, kc * 128:(kc + 1) * 128], ident_f[:])),
                                    r=[xnk, 'ident_f'], w=[bnk[1 + kc // 4]])
                          for kc in range(8):
                              first = (i == 0 and kc == 0)
                              op('act', 'activation', r=[bnk[1 + kc // 4], 'c_A', 'c_sh'] + ([] if first else [hk]),
                                 w=([hk] if first else []) + [(hk, kc, i)], out=hs[:, kc, i * 128:(i + 1) * 128],
                                 in_=bnk[0][kc // 4][:, (kc % 4) * 128:(kc % 4 + 1) * 128], func=AF.Identity,
                                 scale=cols[:, 24 + kc:25 + kc], bias=cols[:, kc:kc + 1])
                      dma('pool', hT_d[c], hs[:], r=[hk] + [(hk, kc_, i_) for kc_ in range(8) for i_ in range(4)], w=[('hTd', c)], sem=hk)

                  chk(2)
                  gspecs = [(C_GA, 4, 0), (C_GB, 2, 4), (C_GC, 4, 6)]
                  gslots = [1, 0, 1]
                  wk_next = load_w(gslots[0], [(w_in[l][:, gspecs[0][0]:gspecs[0][0] + gspecs[0][1] * 128], 0)], l)
                  for gi_, (c0, ntile, row0) in enumerate(gspecs):
                      wk = wk_next
                      gslot = gslots[gi_]
                      if gi_ + 1 < 3:
                          c0n, ntn, _ = gspecs[gi_ + 1]
                          wk_next = load_w(gslots[gi_ + 1], [(w_in[l][:, c0n:c0n + ntn * 128], 0)], l)
                      else:
                          wk_next = load_w(0, [(w_in[l][:, 0:384], 0)], l)
                      for c in range(NCH):
                          ht, hk = load_hc(c)
                          for j in range(ntile):
                              p, pk = proj(ht, hk, gslot, wk, j * 128, 128)
                              og, ok = ogrot.next()
                              op('act', 'activation', r=[pk], w=[ok], out=og[:], in_=p[:], func=AF.Silu)
                              dma('pool', gate_d[c, :, row0 + j, :], og[:], r=[ok],
                                  w=[('gated', row0 + j, c)], sem=ok)

                  chk(3)
                  def latent_norm(ntile, gcol0, width, dst_d, dkey, wk, slot, c, ht, hk, lt, lk):
                      tk = []
                      for j in range(ntile):
                          p, pk = proj(ht, hk, slot, wk, j * 128, 128)
                          KD0 = int(_os.environ.get('KDBG', '9'))
                          if KD0 < 0:
                              continue
                          t1, k1 = tmprot.next()
                          op('dve', 'tensor_copy', r=[pk], w=[k1], out=t1[:], in_=p[:])
                          tk.append((t1, k1))
                          if KD0 < 1:
                              continue
                          pt, ptk = ptrot.next()
                          op('dve', 'tensor_tensor', r=[k1], w=[ptk], out=pt[:], in0=t1[:], in1=t1[:], op=ALU.mult)
                          mm(ssp[:, :], ones_b[:], pt[:], j == 0, j == ntile - 1, r=['ones_b', ptk], w=[('sc', 2)])
                      KD = int(_os.environ.get('KDBG', '9'))
                      if KD < 2:
                          return
                      t2, k2 = tmprot.next()
                      op('act', 'activation', r=[('sc', 2)], w=[k2], out=t2[:], in_=ssp[:], func=AF.Ln, scale=1.0 / width, bias=EPS)
                      if KD < 3:
                          return
                      t3, k3 = tmprot.next()
                      op('act', 'activation', r=[k2], w=[k3], out=t3[:], in_=t2[:], func=AF.Exp, scale=-0.5)
                      if KD < 4:
                          return
                      for j in range(ntile):
                          t1, k1 = tk[j]
                          op('dve', 'scalar_tensor_tensor', r=[k1, k3, 'c_gq', 'c_gkv'], w=[lk], out=lt[:, j, :], in0=t1[:],
                             scalar=cols[:, gcol0 + j:gcol0 + j + 1], in1=t3[:], op0=ALU.mult, op1=ALU.mult)
                      dma('pool', dst_d[c], lt[:, 0:ntile, :], r=[lk],
                          w=[(dkey, c)], sem=lk)

                  wk = wk_next
                  wk_ckv = load_w(1, [(w_in[l][:, 384:672], 0), (w_krp[l], 288)], l)
                  for c in range(NCH):
                      ht, hk = load_hc(c)
                      lt, lk = latcrot.next()
                      latent_norm(3, 32, 384.0, cqn_d, 'cqnd', wk, 0, c, ht, hk, lt, lk)
                  wk = wk_ckv
                  for c in range(NCH):
                      ht, hk = load_hc(c)
                      lt, lk = latkrot.next()
                      latent_norm(2, 36, 256.0, ckvn_d, 'ckvnd', wk, 1, c, ht, hk, lt, lk)
                      rt, rk = r32rot.next()
                      S.add('sp', (lambda e, rt=rt, c=c: [e.dma_start(out=rt[0:32, 0, :], in_=cos32_d[:, c * 512:(c + 1) * 512]),
                                                          e.dma_start(out=rt[0:32, 1, :], in_=ssin32_d[:, c * 512:(c + 1) * 512])]),
                            r=[], w=[rk], dma=2, semkey=rk)
                      pa, pak = proj(ht, hk, 1, wk, 256, 32)
                      pb, pbk = proj(ht, hk, 1, wk, 288, 32)
                      t1, k1 = tmprot.next()
                      op('dve', 'tensor_tensor', r=[pak, rk], w=[k1], out=t1[0:32, :], in0=pa[0:32, :], in1=rt[0:32, 0, :], op=ALU.mult)
                      t2, k2 = tmprot.next()
                      op('dve', 'tensor_tensor', r=[pbk, rk], w=[k2], out=t2[0:32, :], in0=pb[0:32, :], in1=rt[0:32, 1, :], op=ALU.mult)
                      kt_, kk = krcrot.next()
                      op('dve', 'tensor_tensor', r=[k1, k2], w=[kk], out=kt_[:], in0=t1[0:32, :], in1=t2[0:32, :], op=ALU.add)
                      dma('pool', kr_d[:, c * 512:(c + 1) * 512], kt_[:], r=[kk], w=[('krd', c)], sem=kk)

                  prefetch = {}
                  deferred = []
                  prefetch[('dil', 0, 0)] = load_w(0, [(w_in[l][:, C_QKVB:C_QKVB + 128], 0),
                                                       (w_in[l][:, C_QKVB + 768:C_QKVB + 768 + 128], 128),
                                                       (w_in[l][:, C_QKVB + 1536:C_QKVB + 1536 + 128], 256)], l)
                  chk(4)
                  for u in range(4):
                      for c in range(NCH):
                          lt, lk = latkrot.next()
                          dma('sp', lt[:, 0:2, :], ckvn_d[c], r=[('ckvnd', c)], w=[lk], sem=lk)
                          kt_, kk = krcrot.next()
                          dma('sp', kt_[:], kr_d[:, c * 512:(c + 1) * 512], r=[('krd', c)], w=[kk], sem=kk)
                          for e in range(2):
                              h = 2 * u + e
                              p, pk = pjrot.next()
                              for kc in range(2):
                                  mm(p[0:64, :], wukv_s[:, kc, h * 128:h * 128 + 64], lt[:, kc, :], kc == 0, kc == 1, r=['wmla', lk], w=[pk])
                              op('act', 'activation', r=[pk], w=[('KQ', e)], out=KQ[e][0:64, c * 512:(c + 1) * 512], in_=p[0:64, :], func=AF.Copy)
                              op('dve', 'tensor_copy', r=[kk], w=[('KQ', e)], out=KQ[e][64:96, c * 512:(c + 1) * 512], in_=kt_[:])
                              p, pk = pjrot.next()
                              for kc in range(2):
                                  mm(p[0:64, :], wukv_s[:, kc, h * 128 + 64:h * 128 + 128], lt[:, kc, :], kc == 0, kc == 1, r=['wmla', lk], w=[pk])
                              op('act', 'activation', r=[pk], w=[('VTc', c)], out=VT[e * 64:(e + 1) * 64, c * 512:(c + 1) * 512], in_=p[0:64, :], func=AF.Copy)
                          if c > 0:
                              v_transposes_c(c - 1)
                      v_transposes_c(NCH - 1)
                      qres = {}

                      def q_loads(c):
                          lt, lk = latcrot.next()
                          dma('sp', lt[:], cqn_d[c], r=[('cqnd', c)], w=[lk], sem=lk)
                          rt, rk = r32rot.next()
                          S.add('sp', (lambda e_, rt=rt, c=c: [e_.dma_start(out=rt[64:96, 0, :], in_=cos32_d[:, c * 512:(c + 1) * 512]),
                                                               e_.dma_start(out=rt[64:96, 1, :], in_=ssin32_d[:, c * 512:(c + 1) * 512])]),
                                r=[], w=[rk], dma=2, semkey=rk)
                          qres[c] = (lt, lk, rt, rk)

                      def q_prep(c, e):
                          (lt, lk, rt, rk) = qres[c]
                          h = 2 * u + e
                          pa, pak = pjrot.next()
                          for kc in range(3):
                              mm(pa[0:96, :], wuqh_s[:, kc, h * 96:(h + 1) * 96], lt[:, kc, :], kc == 0, kc == 2, r=['wmla', lk], w=[pak])
                          pb, pbk = pjrot.next()
                          for kc in range(3):
                              mm(pb[0:96, :], wuqrp_s[:, kc, h * 96:(h + 1) * 96], lt[:, kc, :], kc == 0, kc == 2, r=['wmla', lk], w=[pbk])
                          qt, qk = qcrot.next()
                          t1, k1 = tmprot.next()
                          op('dve', 'tensor_tensor', r=[pak, rk], w=[k1], out=t1[64:96, :], in0=pa[64:96, :], in1=rt[64:96, 0, :], op=ALU.mult)
                          t2, k2 = tmprot.next()
                          op('dve', 'tensor_tensor', r=[pbk, rk], w=[k2], out=t2[64:96, :], in0=pb[64:96, :], in1=rt[64:96, 1, :], op=ALU.mult)
                          op('dve', 'tensor_tensor', r=[k1, k2], w=[qk], out=qt[64:96, :], in0=t1[64:96, :], in1=t2[64:96, :], op=ALU.add)
                          op('dve', 'tensor_copy', r=[pak], w=[qk], out=qt[0:64, :], in_=pa[0:64, :])
                          return qt, qk

                      jobs = [(c, e) for c in range(NCH) for e in range(2)]
                      q_loads(0)
                      if NCH > 1:
                          q_loads(1)
                      qcur = q_prep(0, 0)
                      gres = None
                      for ji, (c, e) in enumerate(jobs):
                          qnext = None
                          if ji + 1 < len(jobs):
                              c2, e2 = jobs[ji + 1]
                              if e2 == 0 and c2 + 1 < NCH:
                                  q_loads(c2 + 1)
                              qnext = q_prep(c2, e2)
                          if e == 0:
                              gt, gk = load_gate(u, c)
                              og, ok = ogrot.next()
                              gres = (gt, gk, og, ok)
                          (gt, gk, og, ok) = gres
                          qt, qk = qcur
                          blocks = [(KQ[e][0:96, j * 128:(j + 1) * 128], qt[0:96, :], 0, 512, None, Var[:, j, e * 64:e * 64 + 128],
                                     [('KQ', e), qk]) for j in range(NT)]
                          acc, ak = attend(blocks, 96.0 ** -0.5)
                          finish_head(e, acc, acc, [ak], gt, gk, og, ok)
                          if e == 1:
                              store_og(og, ok, u, c)
                          qcur = qnext


                  def na_segs(hh_):
                      return [(w_in[l][:, C_QKVC + hh_ * 64:C_QKVC + hh_ * 64 + 128], 0),
                              (w_in[l][:, C_QKVC + 512 + hh_ * 64:C_QKVC + 512 + hh_ * 64 + 128], 128),
                              (w_in[l][:, C_QKVC + 1024 + hh_ * 64:C_QKVC + 1024 + hh_ * 64 + 128], 256)]
                  for pp in range(2):
                      for gi, dd in enumerate((1, 4, 16)):
                          L = SL // dd
                          hh = gi * 4 + 2 * pp
                          def dil_segs(hh_):
                              return [(w_in[l][:, C_QKVB + hh_ * 64:C_QKVB + hh_ * 64 + 128], 0),
                                      (w_in[l][:, C_QKVB + 768 + hh_ * 64:C_QKVB + 768 + hh_ * 64 + 128], 128),
                                      (w_in[l][:, C_QKVB + 1536 + hh_ * 64:C_QKVB + 1536 + hh_ * 64 + 128], 256)]
                          wk = prefetch.pop(('dil', pp, gi), None)
                          if wk is None:
                              wk = load_w(gi % 2, dil_segs(hh), l)

                          def cm(buf, c, dd=dd, L=L):
                              if dd == 1:
                                  return buf[:, c * 512:(c + 1) * 512]
                              m0 = c * 512 // dd
                              return buf[:, 0:SL].rearrange("p (r m) -> p m r", r=dd)[:, m0:m0 + 512 // dd, :]

                          def nat(t):
                              return t[:] if dd == 1 else t[:].rearrange("p (m r) -> p m r", r=dd)

                          for c in range(NCH):
                              ht, hk = load_hc(c)
                              rt, rk = r64rot.next()
                              S.add('sp', (lambda e_, rt=rt, c=c: [e_.dma_start(out=rt[:, 0, :], in_=cos64_d[:, c * 512:(c + 1) * 512]),
                                                                   e_.dma_start(out=rt[:, 1, :], in_=ssin64_d[:, c * 512:(c + 1) * 512])]),
                                    r=[], w=[rk], dma=2, semkey=rk)
                              for which, dstbuf, dkey in ((0, KQ[1], ('KQ', 1)), (1, KQ[0], ('KQ', 0))):
                                  p, pk = proj(ht, hk, gi % 2, wk, which * 128, 128)
                                  t1, k1 = tmprot.next()
                                  op('dve', 'tensor_tensor', r=[pk, rk], w=[k1], out=t1[:], in0=p[:], in1=rt[:, 0, :], op=ALU.mult)
                                  t2, k2 = tmprot.next()
                                  for blk in range(4):
                                      srcb = blk ^ 1
                                      op('dve', 'tensor_tensor', r=[pk, rk], w=[k2], out=t2[blk * 32:(blk + 1) * 32, :],
                                         in0=p[srcb * 32:(srcb + 1) * 32, :], in1=rt[blk * 32:(blk + 1) * 32, 1, :], op=ALU.mult)
                                  op('pool', 'tensor_tensor', r=[k1, k2], w=[dkey], out=cm(dstbuf, c), in0=nat(t1), in1=nat(t2), op=ALU.add)
                              p, pk = proj(ht, hk, gi % 2, wk, 256, 128)
                              op('act', 'activation', r=[pk], w=['VT'], out=cm(VT, c), in_=(p[:] if dd == 1 else p[:].rearrange("p (m r) -> p m r", r=dd)),
                                 func=AF.Copy)
                          if gi < 2:
                              prefetch[('dil', pp, gi + 1)] = load_w((gi + 1) % 2, dil_segs((gi + 1) * 4 + 2 * pp), l)
                          while deferred:
                              deferred.pop(0)()
                          v_transposes(NT)
                          for c in range(NCH):
                              segs = []
                              lo, hi = c * 512, (c + 1) * 512
                              for rcls in range(lo // L, (hi - 1) // L + 1):
                                  s0, s1 = max(lo, rcls * L), min(hi, (rcls + 1) * L)
                                  segs.append((rcls, s0 - lo, s1 - s0, s0 - rcls * L))
                              for e in range(2):
                                  blocks = []
                                  for (rcls, a, wdt, mq0) in segs:
                                      for j in range(rcls * L // 128, (rcls + 1) * L // 128):
                                          mk0 = j * 128 - rcls * L
                                          if mk0 + 127 < mq0 - 64 or mk0 > mq0 + wdt - 1 + 64:
                                              continue
                                          dl = mk0 - mq0
                                          blocks.append((KQ[0][e * 64:(e + 1) * 64, j * 128:(j + 1) * 128],
                                                         KQ[1][e * 64:(e + 1) * 64, lo + a:lo + a + wdt], a, wdt,
                                                         tm[:, 512 - dl:512 - dl + wdt], Var[:, j, e * 64:e * 64 + 128], [('KQ', 0), ('KQ', 1)]))
                                  acc, ak = attend(blocks, 0.125)
                                  nr = slice(e * 64, e * 64 + 64)
                                  dr = slice(64 - e * 64, 128 - e * 64)
                                  for (rcls, a, wdt, mq0) in segs:
                                      n0 = rcls + dd * mq0
                                      for (accbuf, rows, key) in ((numacc, nr, 'numacc'), (denacc, dr, 'denacc')):
                                          oap = accbuf[rows, n0:n0 + dd * (wdt - 1) + 1:dd]
                                          if gi == 0:
                                              op('dve', 'tensor_copy', r=[ak], w=[key], out=oap, in_=acc[rows, a:a + wdt])
                                          else:
                                              op('dve', 'tensor_tensor', r=[ak, key], w=[key], out=oap, in0=acc[rows, a:a + wdt], in1=oap, op=ALU.add)
                      if pp == 0:
                          prefetch[('dil', 1, 0)] = load_w(0, dil_segs(2), l)
                      else:
                          prefetch[('na', 0)] = load_w(0, na_segs(0), l)
                      def dil_normalize(pp=pp):
                          for c in range(NCH):
                              gt, gk = load_gate(4 + pp, c)
                              og, ok = ogrot.next()
                              for e in range(2):
                                  finish_head(e, numacc[:, c * 512:(c + 1) * 512], denacc[:, c * 512:(c + 1) * 512], ['numacc', 'denacc'], gt, gk, og, ok, split=True)
                              store_og(og, ok, 4 + pp, c)
                      deferred.append(dil_normalize)

                  chk(6)
                  rows = SL // 64
                  for pp in range(4):
                      hh = 2 * pp
                      wk = prefetch.pop(('na', pp), None)
                      if wk is None:
                          wk = load_w(pp % 2, na_segs(hh), l)
                      def etab_step(e, var):
                          h = hh + e
                          base = rpbf[l, h]
                          if var == 0:
                              S.add('sp', (lambda e_, base=base: [e_.dma_start(
                                  out=Xraw[kr * 64:(kr + 1) * 64, :].rearrange("p (s q) -> p s q", q=64),
                                  in_=bass.AP(tensor=base.tensor, offset=base.offset + kr * 127, ap=[[1, 64], [127, 23], [1, 64]])) for kr in range(2)]),
                                    r=[], w=['ogc'], dma=2, semkey='ogc')
                          op('dve', 'scalar_tensor_tensor', r=['ogc', 'cmask'], w=['Xv'], out=Xv[:], in0=Xraw[:], scalar=8.0, in1=cmask[:, var, :],
                             op0=ALU.mult, op1=ALU.add)
                          for q0 in range(0, 23 * 64, 512):
                              q1 = min(q0 + 512, 23 * 64)
                              p, pk = pj4rot.next()
                              mm(p[:, 0:q1 - q0], jmat[:], Xv[:, q0:q1], True, True, r=['jmat', 'Xv'], w=[pk])
                              op('act', 'activation', r=[pk], w=['Etab'], out=Etab[:, e * 2 + var, q0:q1], in_=p[:, 0:q1 - q0], func=AF.Copy)

                      esteps = [(0, 0), (0, 1), (1, 0), (1, 1)]
                      for c in range(NCH):
                          ht, hk = load_hc(c)
                          for which, dstbuf, dkey in ((0, KQ[1], ('KQ', 1)), (1, KQ[0], ('KQ', 0)), (2, VT, ('VTc', c))):
                              p, pk = proj(ht, hk, pp % 2, wk, which * 128, 128)
                              if which == 1:
                                  op('dve', 'tensor_copy', r=[pk], w=[dkey], out=dstbuf[:, c * 512:(c + 1) * 512], in_=p[:])
                              else:
                                  op('act', 'activation', r=[pk], w=[dkey], out=dstbuf[:, c * 512:(c + 1) * 512], in_=p[:], func=AF.Copy)
                          if c > 0:
                              v_transposes_c(c - 1)
                          if esteps:
                              etab_step(*esteps.pop(0))
                      while esteps:
                          etab_step(*esteps.pop(0))
                      if pp < 3:
                          prefetch[('na', pp + 1)] = load_w((pp + 1) % 2, na_segs(2 * (pp + 1)), l)
                      while deferred:
                          deferred.pop(0)()
                      v_transposes_c(NCH - 1)
                      for c in range(NCH):
                          R0 = 8 * c
                          if c == 0:
                              segs = [(0, 4, 1, list(range(0, 4))), (4, 4, 0, None)]
                          elif c == NCH - 1:
                              segs = [(R0, 5, 0, None), (R0 + 5, 3, 1, list(range(R0 // 2, R0 // 2 + 4)))]
                          else:
                              segs = [(R0, 8, 0, None)]
                          gt, gk = load_gate(6 + pp, c)
                          og, ok = ogrot.next()
                          for e in range(2):
                              blocks = []
                              for (qa, nq, var, jl) in segs:
                                  if jl is None:
                                      jl = list(range(max(0, qa - 4) // 2, min(rows - 1, qa + nq + 2) // 2 + 1))
                                  a = (qa - R0) * 64
                                  for j in jl:
                                      s0 = qa - 2 * j + 10
                                      assert 0 <= s0 and s0 + nq <= 23
                                      blocks.append((KQ[0][e * 64:(e + 1) * 64, j * 128:(j + 1) * 128],
                                                     KQ[1][e * 64:(e + 1) * 64, c * 512 + a:c * 512 + a + nq * 64], a, nq * 64,
                                                     Etab[:, e * 2 + (1 - var), s0 * 64:(s0 + nq) * 64], Var[:, j, e * 64:e * 64 + 128],
                                                     [('KQ', 0), ('KQ', 1)]))
                              acc, ak = attend(blocks, 0.125)
                              finish_head(e, acc, acc, [ak], gt, gk, og, ok)
                          store_og(og, ok, 6 + pp, c)

                  chk(7)
                  wp2 = Xv[:, 0:1280].rearrange("p (k n) -> p k n", n=128)
                  for fp in range(4):
                      fs = (2 * fp, 2 * fp + 1)
                      def mg_segs(f_):
                          return [(w_in[l][:, C_MG + b * 1024 + f_ * 128:C_MG + b * 1024 + (f_ + 1) * 128], b * 128) for b in range(3)]

                      def wp_load(f_, wt, key):
                          S.add('pool', (lambda e_, l=l, f=f_, wt=wt: [
                              e_.dma_start(out=wt[:, 0:4, :], in_=kt_view(w_pa[l][:, f * 128:(f + 1) * 128])),
                              e_.dma_start(out=wt[:, 4:6, :], in_=kt_view(w_pb[l][:, f * 128:(f + 1) * 128])),
                              e_.dma_start(out=wt[:, 6:10, :], in_=kt_view(w_pc[l][:, f * 128:(f + 1) * 128]))]),
                                r=[], w=[key], dma=3, semkey=key)
                      wps = (wp_s, wp2)
                      wpk = ('wp', 'Xv')
                      wks = []
                      for j, f in enumerate(fs):
                          pre = prefetch.pop(('mix', fp, j), None)
                          if pre is None:
                              wks.append(load_w(j, mg_segs(f), l))
                              wp_load(f, wps[j], wpk[j])
                          else:
                              wks.append(pre)
                      for j, f in enumerate(()):
                          S.add('pool', (lambda e_, l=l, f=f, wt=wps[j]: [
                              e_.dma_start(out=wt[:, 0:4, :], in_=kt_view(w_pa[l][:, f * 128:(f + 1) * 128])),
                              e_.dma_start(out=wt[:, 4:6, :], in_=kt_view(w_pb[l][:, f * 128:(f + 1) * 128])),
                              e_.dma_start(out=wt[:, 6:10, :], in_=kt_view(w_pc[l][:, f * 128:(f + 1) * 128]))]),
                                r=[], w=[wpk[j]], dma=3, semkey=wpk[j])
                      for c in range(NCH):
                          ht, hk = load_hc(c)
                          dma('sp', ogc[:], og_d[c],
                              r=[('ogd', k, c) for k in range(10)], w=['ogc'], sem='ogc')
                          for j, f in enumerate(fs):
                              mx, mxk = tmprot.next()
                              og, ok = ogrot.next()
                              sgs = {}
                              if j == 0:
                                  for b in range(3):
                                      p, pk = proj(ht, hk, j, wks[j], b * 128, 128)
                                      sg, sgk = tmprot.next()
                                      op('act', 'activation', r=[pk], w=[sgk], out=sg[:], in_=p[:], func=AF.Sigmoid)
                                      sgs[b] = (sg, sgk)
                              for b, (k0, k1) in enumerate(((0, 4), (4, 6), (6, 10))):
                                  if j != 0:
                                      p, pk = proj(ht, hk, j, wks[j], b * 128, 128)
                                      sg, sgk = tmprot.next()
                                      op('act', 'activation', r=[pk], w=[sgk], out=sg[:], in_=p[:], func=AF.Sigmoid)
                                  else:
                                      sg, sgk = sgs[b]
                                  p2, pk2 = pj4rot.next()
                                  for kt in range(k0, k1):
                                      mm(p2[:, :], wps[j][:, kt, :], ogc[:, kt, :], kt == k0, kt == k1 - 1, r=[wpk[j], 'ogc'], w=[pk2])
                                  if b == 0:
                                      op('dve', 'tensor_tensor', r=[pk2, sgk], w=[mxk], out=mx[:], in0=p2[:], in1=sg[:], op=ALU.mult)
                                  else:
                                      tq, tqk = tmprot.next()
                                      op('dve', 'tensor_tensor', r=[pk2, sgk], w=[tqk], out=tq[:], in0=p2[:], in1=sg[:], op=ALU.mult)
                                      if b == 1:
                                          op('dve', 'tensor_tensor', r=[mxk, tqk], w=[mxk], out=mx[:], in0=mx[:], in1=tq[:], op=ALU.add)
                                      else:
                                          op('dve', 'tensor_tensor', r=[mxk, tqk], w=[ok], out=og[:], in0=mx[:], in1=tq[:], op=ALU.add)
                              dma('pool', mix_d[c, :, f, :], og[:], r=[ok], w=[('mixd', f, c)], sem=ok)
                              if c == NCH - 1 and j == 0:
                                  if fp < 3:
                                      prefetch[('mix', fp + 1, 0)] = load_w(0, mg_segs(2 * (fp + 1)), l)
                                      wp_load(2 * (fp + 1), wp_s, 'wp')
                                  else:
                                      prefetch[('out', 0)] = load_w(0, [(w_out[l][:, 0:512], 0)], l)

                  wk0 = prefetch.pop(('out', 0), None)
                  if wk0 is None:
                      wk0 = load_w(0, [(w_out[l][:, 0:512], 0)], l)
                  wk1 = load_w(1, [(w_out[l][:, 512:1024], 0)], l)
                  for c in range(NCH):
                      mc, mk = hcrot.next()
                      dma('sp', mc[:], mix_d[c],
                          r=[('mixd', f, c) for f in range(8)], w=[mk], sem=mk)
                      for i in range(4):
                          t = 4 * c + i
                          xtt, xk = xtrot.next()
                          dma('sp', xtt[:], src_x[t * 128:(t + 1) * 128, :], r=[('y', si, t)], w=[xk], sem=xk)
                          xnt, xnk = xnrot.next()
                          for half, wkk in ((0, wk0), (1, wk1)):
                              for kt in range(8):
                                  mm(pj[half][:, :], mc[:, kt, i * 128:(i + 1) * 128], wA[half][:, kt, :], kt == 0, kt == 7, r=[mk, wkk], w=[('pj', half)])
                              op('act', 'activation', r=[('pj', half)], w=[xnk], out=xnt[:, half * 512:(half + 1) * 512], in_=pj[half][:, :], func=AF.Copy)
                              op('dve', 'scalar_tensor_tensor', r=[xnk], w=[xnk, ('ssh', half)], out=xnt[:, half * 512:(half + 1) * 512],
                                 in0=xnt[:, half * 512:(half + 1) * 512], scalar=1.0, in1=xnt[:, half * 512:(half + 1) * 512],
                                 op0=ALU.mult, op1=ALU.mult, accum_out=small[:, 4 + half:5 + half])
                          op('dve', 'tensor_tensor', r=[('ssh', 0), ('ssh', 1)], w=['ssq2'], out=small[:, 6:7], in0=small[:, 4:5], in1=small[:, 5:6], op=ALU.add)
                          op('act', 'activation', r=['ssq2'], w=['sd2'], out=small[:, 7:8], in_=small[:, 6:7], func=AF.Sqrt, scale=1.0 / D, bias=EPS)
                          op('dve', 'reciprocal', r=['sd2'], w=['rs2'], out=small[:, 8:9], in_=small[:, 7:8])
                          for half in range(2):
                              op('dve', 'scalar_tensor_tensor', r=[('pj', half), 'rs2', 'G_bc'], w=[xnk], out=xnt[:, half * 512:(half + 1) * 512],
                                 in0=pj[half][:, :], scalar=small[:, 8:9], in1=G_bc[:, half * 512:(half + 1) * 512], op0=ALU.mult, op1=ALU.mult)
                          op('pool', 'tensor_tensor', r=[xnk, xk], w=[xk], out=xtt[:], in0=xnt[:], in1=xtt[:], op=ALU.add)
                          dma('pool', ydst[t * 128:(t + 1) * 128, :], xtt[:], r=[xk], w=[('y', si, t)], sem=xk)

        except _Stop:
            pass
        S.add('sp', (lambda e: None), r=[('y', si, t) for si, SL in enumerate(seq_lens) for t in range(SL // 128)])
        S.add('pool', (lambda e: None), r=[('y', si, t) for si, SL in enumerate(seq_lens) for t in range(SL // 128)])
        run_stream, nsem = S.emit(nc, lambda name: es.enter_context(nc.semaphore(name)))
        with nc.Block() as block:
            @block.sync
            def _(e):
                run_stream('sp', e)

            @block.scalar
            def _(e):
                run_stream('act', e)

            @block.vector
            def _(e):
                run_stream('dve', e)

            @block.gpsimd
            def _(e):
                run_stream('pool', e)

            @block.tensor
            def _(e):
                run_stream('pe', e)
    build_program.stats = STATS
    return nc, len(S.ops)


def host_consts(Smax):
    bf = ml_dtypes.bfloat16
    out = {}
    out["ident_f"] = np.eye(128, dtype=np.float32)
    out["ident_b"] = np.eye(128, dtype=np.float32).astype(bf)
    out["jmat"] = np.eye(128, dtype=np.float32)[::-1].copy().astype(bf)
    out["ones_b"] = np.ones((128, 128), dtype=np.float32).astype(bf)
    pos = np.arange(Smax, dtype=np.float32)
    inv64 = (np.float32(10000.0) ** (-np.arange(32, dtype=np.float32) * np.float32(2.0) / np.float32(64))).astype(np.float32)
    ang = (pos[None, :] * inv64[:, None]).astype(np.float32)
    c64, s64 = np.cos(ang).astype(np.float32), np.sin(ang).astype(np.float32)
    cos64 = np.zeros((128, Smax), np.float32); ssin64 = np.zeros((128, Smax), np.float32)
    for p in range(128):
        cos64[p] = c64[p % 32]
        ssin64[p] = -s64[p % 32] if (p % 64) < 32 else s64[p % 32]
    out["cos64"], out["ssin64"] = cos64, ssin64
    inv32 = (np.float32(10000.0) ** (-np.arange(16, dtype=np.float32) * np.float32(2.0) / np.float32(32))).astype(np.float32)
    ang = (pos[None, :] * inv32[:, None]).astype(np.float32)
    c32, s32 = np.cos(ang).astype(np.float32), np.sin(ang).astype(np.float32)
    cos32 = np.zeros((32, Smax), np.float32); ssin32 = np.zeros((32, Smax), np.float32)
    for p in range(32):
        cos32[p] = c32[p % 16]
        ssin32[p] = -s32[p % 16] if p < 16 else s32[p % 16]
    out["cos32"], out["ssin32"] = cos32, ssin32
    kp = np.arange(128)[:, None]
    u = np.arange(1152)[None, :] - 512
    out["tm"] = np.where(np.abs(kp - u) <= 64, 0.0, NEG).astype(np.float32).astype(bf)
    cm = np.zeros((128, 2, 23, 64), np.float32)
    for pr in range(128):
        krp, kcp = pr // 64, pr % 64
        kcol = 63 - kcp
        for sp_ in range(23):
            roff = 18 - (sp_ + krp)
            for var, (lo, hi) in enumerate(((0, 14), (3, 10))):
                rv = lo <= roff <= hi
                qcol = np.arange(64)
                cs = np.clip(qcol - 8, 0, 48)
                cv = (kcol >= cs) & (kcol < cs + 16)
                cm[pr, var, sp_, :] = np.where(cv & rv, 0.0, NEG)
    out["cmask"] = cm.reshape(128, 2, 23 * 64).astype(bf)
    return out


def host_weights(w):
    out = {}
    depth = w["w_in"].shape[0]
    for k in ("w_ada", "b_ada", "g_pre", "g_post", "w_in", "g_q", "g_kv", "w_ukv", "w_pa", "w_pb", "w_pc", "w_out"):
        out[k] = np.ascontiguousarray(w[k], dtype=np.float32)
    perm = np.concatenate([np.arange(16, 32), np.arange(0, 16)])
    out["w_krp"] = np.ascontiguousarray(w["w_in"][:, :, 640 + perm])
    uq = np.asarray(w["w_uq"]).reshape(depth, 384, 8, 96)
    out["w_uqh"] = np.ascontiguousarray(uq.reshape(depth, 384, 768))
    out["w_uqrp"] = np.ascontiguousarray(np.concatenate([uq[..., 0:64], uq[..., 64 + perm]], axis=-1).reshape(depth, 384, 768))
    rp = np.asarray(w["rpb"])
    rf = np.zeros((depth, 8, 24, 127), np.float32)
    for u in range(4, 19):
        rf[:, :, u, 48:79] = rp[:, :, 18 - u, ::-1]
    out["rpbf"] = rf
    return out


_PROG_CACHE = {}


def run_cores(seq_lens, depth, per_core_inputs):
    key = (tuple(seq_lens), depth)
    if key not in _PROG_CACHE:
        _PROG_CACHE[key] = build_program(list(seq_lens), depth)[0]
    nc = _PROG_CACHE[key]
    res = run_bass_kernel_spmd(nc, per_core_inputs, core_ids=list(range(len(per_core_inputs))))
    return res.results


def kernel(x_prompt, x_sample, c_prompt, c_sample, w_ada, b_ada, g_pre, g_post, w_in, g_q, w_uq, g_kv, w_ukv, rpb,
           w_pa, w_pb, w_pc, w_out):
    ncore = 8
    x_prompt = np.asarray(x_prompt, dtype=np.float32); x_sample = np.asarray(x_sample, dtype=np.float32)
    c_prompt = np.asarray(c_prompt, dtype=np.float32); c_sample = np.asarray(c_sample, dtype=np.float32)
    depth = np.asarray(w_in).shape[0]
    W = host_weights(dict(w_ada=np.asarray(w_ada), b_ada=np.asarray(b_ada), g_pre=np.asarray(g_pre), g_post=np.asarray(g_post),
                          w_in=np.asarray(w_in), g_q=np.asarray(g_q), w_uq=np.asarray(w_uq), g_kv=np.asarray(g_kv),
                          w_ukv=np.asarray(w_ukv), rpb=np.asarray(rpb), w_pa=np.asarray(w_pa), w_pb=np.asarray(w_pb),
                          w_pc=np.asarray(w_pc), w_out=np.asarray(w_out)))
    seq_lens = [x_sample.shape[1], x_prompt.shape[1], x_prompt.shape[1]]
    C = host_consts(max(seq_lens))
    in_maps = []
    for i in range(ncore):
        m = dict(W)
        m.update(C)
        m["x0"] = np.ascontiguousarray(x_sample[i]); m["x1"] = np.ascontiguousarray(x_prompt[2 * i]); m["x2"] = np.ascontiguousarray(x_prompt[2 * i + 1])
        m["c"] = np.ascontiguousarray(np.stack([c_sample[i], c_prompt[2 * i], c_prompt[2 * i + 1]]))
        in_maps.append(m)
    res = run_cores(seq_lens, depth, in_maps)
    y_prompt = np.empty_like(x_prompt); y_sample = np.empty_like(x_sample)
    for i in range(ncore):
        y_sample[i] = res[i]["y0"]; y_prompt[2 * i] = res[i]["y1"]; y_prompt[2 * i + 1] = res[i]["y2"]
    return (y_prompt, y_sample)
```

```python
import contextlib
import numpy as np
import ml_dtypes
import concourse.bass as bass
import concourse.mybir as mybir
from concourse.bass_utils import run_bass_kernel_spmd

F32 = mybir.dt.float32
BF16 = mybir.dt.bfloat16
AF = mybir.ActivationFunctionType
ALU = mybir.AluOpType
NEG = -1e30
EPS = 1e-6
D = 1024
DIN = 8864
C_GA, C_QKVB, C_GB, C_QKVC, C_GC, C_MG = 672, 1184, 3488, 3744, 5280, 5792


class Sched:
    ENGS = ('pe', 'act', 'dve', 'pool', 'sp')

    def __init__(self):
        self.ops = []
        self.last_w = {}
        self.readers = {}

    def add(self, eng, fn, r=(), w=(), dma=0, semkey=None):
        deps = set()
        ops = self.ops
        for k in r:
            lw = self.last_w.get(k)
            if lw is not None:
                deps.add(lw)
        for k in w:
            lw = self.last_w.get(k)
            if lw is not None:
                deps.add(lw)
            for x in self.readers.get(k, ()):
                deps.add(x)
        oid = len(ops)
        ops.append([eng, fn, deps, dma, semkey])
        for k in r:
            lst = self.readers.setdefault(k, [])
            if not dma:
                lst[:] = [x for x in lst if ops[x][3] or ops[x][0] != eng]
            lst.append(oid)
        for k in w:
            self.last_w[k] = oid
            self.readers[k] = []
        return oid

    def emit(self, nc, semctx):
        ops = self.ops
        n = len(ops)
        needed = [False] * n
        for o in ops:
            for d in o[2]:
                od = ops[d]
                if (not od[3]) and od[0] == 'pe' and o[0] == 'pe' and not o[3]:
                    continue
                needed[d] = True
        sig = [None] * n
        cnt = {}
        for i, o in enumerate(ops):
            if o[3]:
                key = ('dma', o[0], o[4])
                cnt[key] = cnt.get(key, 0) + 16 * o[3]
                sig[i] = (key, cnt[key])
            elif needed[i]:
                key = ('eng', o[0])
                cnt[key] = cnt.get(key, 0) + 1
                sig[i] = (key, cnt[key])
        sems = {key: semctx("s%d" % j) for j, key in enumerate(cnt)}
        streams = {e: [] for e in self.ENGS}
        for i, o in enumerate(ops):
            streams[o[0]].append(i)

        def run_stream(ename, eo):
            waited = {}
            for i in streams[ename]:
                o = ops[i]
                need = {}
                for d in o[2]:
                    od = ops[d]
                    if (not od[3]) and od[0] == 'pe' and ename == 'pe' and not o[3]:
                        continue
                    k, v = sig[d]
                    if need.get(k, 0) < v:
                        need[k] = v
                for k, v in need.items():
                    if waited.get(k, 0) < v:
                        eo.wait_ge(sems[k], v)
                        waited[k] = v
                res = o[1](eo)
                if o[3]:
                    if not isinstance(res, (list, tuple)):
                        res = [res]
                    assert len(res) == o[3]
                    for ins in res:
                        ins.then_inc(sems[sig[i][0]], 16)
                elif needed[i]:
                    res.then_inc(sems[sig[i][0]], 1)
        return run_stream, len(sems)


class Rot:
    def __init__(self, name, tiles, keys=None):
        self.name, self.tiles, self.i = name, tiles, 0
        self.keys = keys if keys is not None else [(name, j) for j in range(len(tiles))]

    def next(self):
        j = self.i % len(self.tiles)
        self.i += 1
        return self.tiles[j], self.keys[j]


def build_program(seq_lens, depth):
    nseq = len(seq_lens)
    Smax = max(seq_lens)
    Tmax = Smax // 128
    nc = bass.Bass("TRN2", target_bir_lowering=False)
    es = contextlib.ExitStack()

    def din(name, shape, dt=F32):
        return nc.dram_tensor(name, list(shape), dt, kind="ExternalInput").ap()

    def dscr(name, shape, dt):
        return nc.dram_tensor(name, list(shape), dt, kind="Internal").ap()

    xs = [din("x%d" % i, [S, D]) for i, S in enumerate(seq_lens)]
    ys = [nc.dram_tensor("y%d" % i, [S, D], F32, kind="ExternalOutput").ap() for i, S in enumerate(seq_lens)]
    c_d = din("c", [nseq, D])
    w_ada = din("w_ada", [depth, D, 3 * D]); b_ada = din("b_ada", [depth, 3 * D])
    g_pre = din("g_pre", [depth, D]); g_post = din("g_post", [depth, D])
    w_in = din("w_in", [depth, D, DIN]); w_krp = din("w_krp", [depth, D, 32])
    g_q = din("g_q", [depth, 384]); w_uqh = din("w_uqh", [depth, 384, 768]); w_uqrp = din("w_uqrp", [depth, 384, 768])
    g_kv = din("g_kv", [depth, 256]); w_ukv = din("w_ukv", [depth, 256, 1024])
    rpbf = din("rpbf", [depth, 8, 24, 127])
    w_pa = din("w_pa", [depth, 512, D]); w_pb = din("w_pb", [depth, 256, D]); w_pc = din("w_pc", [depth, 512, D])
    w_out = din("w_out", [depth, D, D])
    identf_d = din("ident_f", [128, 128]); identb_d = din("ident_b", [128, 128], BF16)
    jmat_d = din("jmat", [128, 128], BF16); onesb_d = din("ones_b", [128, 128], BF16)
    cos64_d = din("cos64", [128, Smax]); ssin64_d = din("ssin64", [128, Smax])
    cos32_d = din("cos32", [32, Smax]); ssin32_d = din("ssin32", [32, Smax])
    tm_d = din("tm", [128, 1152], BF16); cmask_d = din("cmask", [128, 2, 23 * 64], BF16)

    NCHM = Smax // 512
    hT_d = dscr("hT_d", [NCHM, 128, 8, 512], BF16); gate_d = dscr("gate_d", [NCHM, 128, 10, 512], BF16)
    og_d = dscr("og_d", [NCHM, 128, 10, 512], BF16); mix_d = dscr("mix_d", [NCHM, 128, 8, 512], BF16)
    cqn_d = dscr("cqn_d", [NCHM, 128, 3, 512], BF16); ckvn_d = dscr("ckvn_d", [NCHM, 128, 2, 512], BF16)
    kr_d = dscr("kr_d", [32, Smax], BF16); ada_d = dscr("ada_d", [depth, nseq, 3 * D], F32)

    def sb(name, shape, dt=F32):
        return es.enter_context(nc.sbuf_tensor(name, list(shape), dt))

    def ps(name, shape, dt=F32):
        return es.enter_context(nc.psum_tensor(name, list(shape), dt))

    with es:
        ident_f = sb("ident_f_s", [128, 128]); ident_b = sb("ident_b_s", [128, 128], BF16)
        jmat = sb("jmat_s", [128, 128], BF16); ones_b = sb("ones_b_s", [128, 128], BF16)
        tm = sb("tm_s", [128, 1152], BF16); cmask = sb("cmask_s", [128, 2, 23 * 64], BF16)
        wA = [sb("wA%d" % i, [128, 8, 512], BF16) for i in range(2)]
        hc = [sb("hc%d" % i, [128, 8, 512], BF16) for i in range(2)]
        KQ = [sb("KQ%d" % i, [128, Smax], BF16) for i in range(2)]
        Var = sb("Var", [128, Tmax, 192], BF16)
        VT = sb("VT", [128, Smax], BF16)
        PTb = [sb("PT%d" % i, [128, 512], BF16) for i in range(4)]
        ogt = [sb("ogt%d" % i, [128, 512], BF16) for i in range(2)]
        gtile = [sb("gt%d" % i, [128, 512], BF16) for i in range(2)]
        tmpf = [sb("tmp%d" % i, [128, 512]) for i in range(6)]
        xt = [sb("xt%d" % i, [128, D]) for i in range(2)]
        xn = [sb("xn%d" % i, [128, D]) for i in range(1)]
        G_bc = sb("G_bc", [128, D])
        cols = sb("cols", [128, 64])
        small = sb("small", [128, 16])
        numacc = sb("numacc", [128, Smax]); denacc = sb("denacc", [128, Smax])
        latc = [sb("latc%d" % i, [128, 3, 512], BF16) for i in range(2)]
        krc = [sb("krc%d" % i, [32, 512], BF16) for i in range(2)]
        Qc = [sb("Qc%d" % i, [96, 512], BF16) for i in range(2)]
        r64 = [sb("r64_%d" % i, [128, 2, 512]) for i in range(2)]
        wukv_s = sb("wukv_s", [128, 2, 1024], BF16); wuqh_s = sb("wuqh_s", [128, 3, 768], BF16)
        wuqrp_s = sb("wuqrp_s", [128, 3, 768], BF16)
        Etab = sb("Etab", [128, 4, 23 * 64], BF16)
        Xv = sb("Xv", [128, 23 * 64], BF16)
        wp_s = sb("wp_s", [128, 10, 128], BF16)
        ogc = sb("ogc", [128, 10, 512], BF16)
        Xraw = ogc[:, 0:6, :].rearrange("p a b -> p (a b)").bitcast(F32)[:, 0:23 * 64]
        cact = sb("cact", [128, 8, 4])
        bada_t = sb("bada_t", [4, 512]); adarow_t = sb("adarow_t", [4, 512])

        pj = [ps("pj%d" % i, [128, 512]) for i in range(2)]
        scp = [ps("sc%d" % i, [128, 512]) for i in range(3)]
        acp = [ps("ac%d" % i, [128, 512]) for i in range(2)]
        ssp = scp[2]
        tb = ps("tb", [128, 4, 128], BF16)

        S = Sched()
        pjrot = Rot('pj', pj); pj4rot = Rot('pj4', pj + acp, [('pj', 0), ('pj', 1), ('ac', 0), ('ac', 1)]); scrot = Rot('sc', scp); acrot = Rot('ac', acp); ptrot = Rot('pt', PTb)
        hcrot = Rot('hc', hc); xtrot = Rot('xt', xt); xnrot = Rot('xn', xn); tmprot = Rot('tmp', tmpf)
        ogrot = Rot('ogt', ogt); gtrot = Rot('gt', gtile); latcrot = Rot('latc', latc); latkrot = latcrot
        krcrot = Rot('krc', krc); qcrot = Rot('Qc', Qc); r64rot = Rot('r64', r64); r32rot = r64rot

        STATS = {}
        def _stat(eng, tag, n):
            d = STATS.setdefault((eng, tag), [0, 0])
            d[0] += 1; d[1] += n
        def _fsz(ap):
            n = 1
            for x in ap.shape[1:]:
                n *= x
            return n
        def op(eng, meth, r=(), w=(), **kw):
            o = kw.get('out', kw.get('ap'))
            _stat(eng, meth + ('_' + str(kw['func']).split('.')[-1] if 'func' in kw else ''), _fsz(o))
            S.add(eng, (lambda e: getattr(e, meth)(**kw)), r, w)

        def dma(eng, out, in_, r=(), w=(), sem=None, **kw):
            S.add(eng, (lambda e: e.dma_start(out=out, in_=in_, **kw)), r, w, dma=1, semkey=sem)

        def mm(out, lhsT, rhs, start, stop, r, w):
            _stat('pe', 'mm', _fsz(out))
            S.add('pe', (lambda e: e.matmul(out, lhsT=lhsT, rhs=rhs, start=start, stop=stop)), r, w)

        def kt_view(ap2d):
            return ap2d.rearrange("(k p) n -> p k n", p=128)

        for (dst, src, k) in [(ident_f, identf_d, 'ident_f'), (ident_b, identb_d, 'ident_b'), (jmat, jmat_d, 'jmat'),
                              (ones_b, onesb_d, 'ones_b'), (tm, tm_d, 'tm'), (cmask, cmask_d, 'cmask')]:
            dma('sp', dst[:], src, w=[k], sem=k)
        op('dve', 'memset', w=['Var_ones'], ap=Var[:, :, 64:128], constant=1.0)

        S.add('sp', (lambda e: [e.dma_start(out=cact[:, :, s_], in_=c_d[s_].rearrange("(k p) -> p k", p=128),
                                            allow_slow_non_contiguous=True) for s_ in range(nseq)]),
              r=[], w=['cact'], dma=nseq, semkey='cact')
        op('act', 'activation', r=['cact'], w=['cact'], out=cact[:, :, 0:nseq], in_=cact[:, :, 0:nseq], func=AF.Silu)
        for l in range(depth):
            for cb in range(6):
                bada_s, badk = bada_t, 'bada_t'
                adarow, adak = adarow_t, 'adarow_t'
                dma('sp', bada_s[0:nseq, :], b_ada[l, cb * 512:(cb + 1) * 512].partition_broadcast(nseq), r=[], w=[badk], sem=badk)
                for kc in range(8):
                    wt, wk = tmprot.next()
                    dma('sp', wt[:], w_ada[l, kc * 128:(kc + 1) * 128, cb * 512:(cb + 1) * 512], w=[wk], sem=wk)
                    mm(ssp[0:nseq, :], cact[:, kc, 0:nseq], wt[:], kc == 0, kc == 7, r=['cact', wk], w=[('sc', 2)])
                op('dve', 'tensor_tensor', r=[('sc', 2), badk], w=[adak], out=adarow[0:nseq, :], in0=ssp[0:nseq, :],
                   in1=bada_s[0:nseq, :], op=ALU.add)
                dma('pool', ada_d[l, :, cb * 512:(cb + 1) * 512], adarow[0:nseq, :], r=[adak], w=[('adad', l)], sem=adak)

        def load_w(slot, segs, l):
            wk = ('wA', slot)
            fns = []
            for (src, off) in segs:
                n = src.shape[1]
                fns.append((wA[slot][:, :, off:off + n], kt_view(src)))
            S.add('pool', (lambda e: [e.dma_start(out=o, in_=i) for (o, i) in fns]), r=[], w=[wk], dma=len(fns), semkey=wk)
            return wk

        def load_hc(c):
            t, k = hcrot.next()
            dma('sp', t[:], hT_d[c], r=[('hTd', c)], w=[k], sem=k)
            return t, k

        def proj(ht, hk, slot, wk, off, m):
            p, pk = pj4rot.next()
            for kc in range(8):
                mm(p[0:m, :], wA[slot][:, kc, off:off + m], ht[:, kc, :], kc == 0, kc == 7, r=[wk, hk], w=[pk])
            return p, pk

        def attend(blocks, scale, look=2):
            acc, ak = acrot.next()
            nb = {}
            for b in blocks:
                nb[b[2]] = nb.get(b[2], 0) + 1
            seen = {}
            pend = []

            def pv(item):
                (a, wdt, Vap, pt, pk) = item
                seen[a] = seen.get(a, 0) + 1
                mm(acc[:, a:a + wdt], Vap, pt[:, a:a + wdt], seen[a] == 1, seen[a] == nb[a], r=[pk, 'Var', 'Var_ones'], w=[ak])

            for (Kap, Qap, a, wdt, mask_ap, Vap, rkeys) in blocks:
                sp_, sk = scrot.next()
                mm(sp_[:, a:a + wdt], Kap, Qap, True, mask_ap is None, r=rkeys, w=[sk])
                if mask_ap is not None:
                    mm(sp_[:, a:a + wdt], ident_b[:], mask_ap, False, True, r=['ident_b', 'tm', 'Etab'], w=[sk])
                pt, pk = ptrot.next()
                op('act', 'activation', r=[sk], w=[pk], out=pt[:, a:a + wdt], in_=sp_[:, a:a + wdt], func=AF.Exp, scale=scale)
                pend.append((a, wdt, Vap, pt, pk))
                if len(pend) > look:
                    pv(pend.pop(0))
            while pend:
                pv(pend.pop(0))
            return acc, ak

        def finish_head(e, num_ap, den_ap, rk, gt, gk, og, ok, split=False):
            nr = slice(e * 64, e * 64 + 64)
            dr = slice(64 - e * 64, 128 - e * 64)
            t2, k2 = tmprot.next()
            if split:
                t1, k1 = tmprot.next()
                op('act', 'activation', r=rk, w=[k1], out=t1[nr, :], in_=den_ap[dr, :], func=AF.Ln)
                op('act', 'activation', r=[k1], w=[k2], out=t2[nr, :], in_=t1[nr, :], func=AF.Exp, scale=-1.0)
                t3, k3 = tmprot.next()
                op('pool', 'tensor_tensor', r=rk + [k2], w=[k3], out=t3[nr, :], in0=num_ap[nr, :], in1=t2[nr, :], op=ALU.mult)
                op('pool', 'tensor_tensor', r=[k3, gk], w=[ok], out=og[nr, :], in0=t3[nr, :], in1=gt[nr, :], op=ALU.mult)
                return
            if True:
                op('dve', 'reciprocal', r=rk, w=[k2], out=t2[nr, :], in_=den_ap[dr, :])
                t3, k3 = tmprot.next()
                op('dve', 'tensor_tensor', r=rk + [k2], w=[k3], out=t3[nr, :], in0=num_ap[nr, :], in1=t2[nr, :], op=ALU.mult)
            op('dve', 'tensor_tensor', r=[k3, gk], w=[ok], out=og[nr, :], in0=t3[nr, :], in1=gt[nr, :], op=ALU.mult)

        def v_transposes(nt, t_begin=0):
            for t0 in range(t_begin, nt, 4):
                for i in range(4):
                    S.add('pe', (lambda en, i=i, t0=t0: en.transpose(tb[:, i, :], VT[:, (t0 + i) * 128:(t0 + i + 1) * 128], ident_b[:])),
                          r=['VT', 'ident_b'], w=['tb'])
                op('dve', 'tensor_copy', r=['tb'], w=['Var'], out=Var[:, t0:t0 + 4, 0:64], in_=tb[:, :, 0:64])
                op('act', 'activation', r=['tb'], w=['Var'], out=Var[:, t0:t0 + 4, 128:192], in_=tb[:, :, 64:128], func=AF.Copy)

        def v_transposes_c(c):
            t0 = 4 * c
            for i in range(4):
                S.add('pe', (lambda en, i=i, t0=t0: en.transpose(tb[:, i, :], VT[:, (t0 + i) * 128:(t0 + i + 1) * 128], ident_b[:])),
                      r=[('VTc', c), 'VT', 'ident_b'], w=['tb'])
            op('dve', 'tensor_copy', r=['tb'], w=['Var'], out=Var[:, t0:t0 + 4, 0:64], in_=tb[:, :, 0:64])
            op('act', 'activation', r=['tb'], w=['Var'], out=Var[:, t0:t0 + 4, 128:192], in_=tb[:, :, 64:128], func=AF.Copy)

        def load_gate(rowtile, c):
            gt, gk = gtrot.next()
            dma('sp', gt[:], gate_d[c, :, rowtile, :], r=[('gated', rowtile, c)], w=[gk], sem=gk)
            return gt, gk

        def store_og(og, ok, rowtile, c):
            dma('pool', og_d[c, :, rowtile, :], og[:], r=[ok], w=[('ogd', rowtile, c)], sem=ok)

        def rope64_evac(p, pk, c, rt, rk, out_ap, outkeys):
            t1, k1 = tmprot.next()
            op('dve', 'tensor_tensor', r=[pk, rk], w=[k1], out=t1[:], in0=p[:], in1=rt[:, 0, :], op=ALU.mult)
            t2, k2 = tmprot.next()
            for blk in range(4):
                src = blk ^ 1
                op('dve', 'tensor_tensor', r=[pk, rk], w=[k2], out=t2[blk * 32:(blk + 1) * 32, :],
                   in0=p[src * 32:(src + 1) * 32, :], in1=rt[blk * 32:(blk + 1) * 32, 1, :], op=ALU.mult)
            op('dve', 'tensor_tensor', r=[k1, k2], w=outkeys, out=out_ap, in0=t1[:], in1=t2[:], op=ALU.add)

        import os as _os
        STOP = int(_os.environ.get('KSTOP', '99'))
        class _Stop(Exception):
            pass
        def chk(k):
            if STOP < k:
                raise _Stop()
        try:
          for si, SL in enumerate(seq_lens):
              NCH = SL // 512
              NT = SL // 128
              xsrc = xs[si]
              ydst = ys[si]
              for l in range(depth):
                  src_x = xsrc if l == 0 else ydst
                  adl = ada_d[l, si]
                  dma('sp', cols[:, 0:8], adl[0:D].rearrange("(k p) -> p k", p=128), r=[('adad', l)], w=['c_sh'], sem='c_sh', allow_slow_non_contiguous=True)
                  dma('sp', cols[:, 8:16], adl[D:2 * D].rearrange("(k p) -> p k", p=128), r=[('adad', l)], w=['c_sc'], sem='c_sc', allow_slow_non_contiguous=True)
                  dma('sp', cols[:, 16:24], g_pre[l].rearrange("(k p) -> p k", p=128), w=['c_gp'], sem='c_gp', allow_slow_non_contiguous=True)
                  dma('sp', cols[:, 32:35], g_q[l].rearrange("(k p) -> p k", p=128), w=['c_gq'], sem='c_gq', allow_slow_non_contiguous=True)
                  dma('sp', cols[:, 36:38], g_kv[l].rearrange("(k p) -> p k", p=128), w=['c_gkv'], sem='c_gkv', allow_slow_non_contiguous=True)
                  op('dve', 'scalar_tensor_tensor', r=['c_sc', 'c_gp'], w=['c_A'], out=cols[:, 24:32], in0=cols[:, 8:16], scalar=1.0,
                     in1=cols[:, 16:24], op0=ALU.add, op1=ALU.mult)
                  dma('sp', G_bc[:], adl[2 * D:3 * D].partition_broadcast(128), r=[('adad', l)], w=['G_bc'], sem='G_bc')
                  dma('sp', xn[0][:], g_post[l].partition_broadcast(128), w=[('xn', 0)], sem='gp_bc')
                  op('dve', 'tensor_tensor', r=['G_bc', ('xn', 0)], w=['G_bc'], out=G_bc[:], in0=G_bc[:], in1=xn[0][:], op=ALU.mult)
                  S.add('pool', (lambda e, l=l: [e.dma_start(out=wukv_s[:], in_=kt_view(w_ukv[l])),
                                                 e.dma_start(out=wuqh_s[:], in_=kt_view(w_uqh[l])),
                                                 e.dma_start(out=wuqrp_s[:], in_=kt_view(w_uqrp[l]))]),
                        r=[], w=['wmla'], dma=3, semkey='wmla')

                  chk(1)
                  for c in range(NCH):
                      hs, hk = hcrot.next()
                      for i in range(4):
                          t = 4 * c + i
                          xtt, xk = xtrot.next()
                          dma('sp', xtt[:], src_x[t * 128:(t + 1) * 128, :], r=[('y', si, t)], w=[xk], sem=xk)
                          xnt, xnk = xnrot.next()
                          ta, tak = tmprot.next()
                          tbb, tbk = tmprot.next()
                          op('dve', 'scalar_tensor_tensor', r=[xk], w=[tak, 'ssqa'], out=ta[:], in0=xtt[:, 0:512], scalar=1.0, in1=xtt[:, 0:512],
                             op0=ALU.mult, op1=ALU.mult, accum_out=small[:, 0:1])
                          op('dve', 'scalar_tensor_tensor', r=[xk], w=[tbk, 'ssqb'], out=tbb[:], in0=xtt[:, 512:1024], scalar=1.0, in1=xtt[:, 512:1024],
                             op0=ALU.mult, op1=ALU.mult, accum_out=small[:, 3:4])
                          op('dve', 'tensor_tensor', r=['ssqa', 'ssqb'], w=['ssq'], out=small[:, 9:10], in0=small[:, 0:1], in1=small[:, 3:4], op=ALU.add)
                          op('act', 'activation', r=['ssq'], w=['sd'], out=small[:, 1:2], in_=small[:, 9:10], func=AF.Sqrt, scale=1.0 / D, bias=EPS)
                          op('dve', 'reciprocal', r=['sd'], w=['rs'], out=small[:, 2:3], in_=small[:, 1:2])
                          op('dve', 'tensor_scalar', r=[xk, 'rs'], w=[xnk], out=xnt[:], in0=xtt[:], scalar1=small[:, 2:3], scalar2=None, op0=ALU.mult)
                          bnk = (pj, ('pj', 0), ('pj', 1)) if t % 2 == 0 else (acp, ('ac', 0), ('ac', 1))
                          for kc in range(8):
                              S.add('pe', (lambda en, kc=kc, xnt=xnt, bb=bnk[0]: en.transpose(bb[kc // 4][:, (kc % 4) * 128:(kc % 4 + 1) * 128],
                                                                                    xnt[:, kc * 128:(kc + 1) * 128], ident_f[:])),
                                    r=[xnk, 'ident_f'], w=[bnk[1 + kc // 4]])
                          for kc in range(8):
                              first = (i == 0 and kc == 0)
                              op('act', 'activation', r=[bnk[1 + kc // 4], 'c_A', 'c_sh'] + ([] if first else [hk]),
                                 w=([hk] if first else []) + [(hk, kc, i)], out=hs[:, kc, i * 128:(i + 1) * 128],
                                 in_=bnk[0][kc // 4][:, (kc % 4) * 128:(kc % 4 + 1) * 128], func=AF.Identity,
                                 scale=cols[:, 24 + kc:25 + kc], bias=cols[:, kc:kc + 1])
                      dma('pool', hT_d[c], hs[:], r=[hk] + [(hk, kc_, i_) for kc_ in range(8) for i_ in range(4)], w=[('hTd', c)], sem=hk)

                  chk(2)
                  gspecs = [(C_GA, 4, 0), (C_GB, 2, 4), (C_GC, 4, 6)]
                  gslots = [1, 0, 1]
                  wk_next = load_w(gslots[0], [(w_in[l][:, gspecs[0][0]:gspecs[0][0] + gspecs[0][1] * 128], 0)], l)
                  for gi_, (c0, ntile, row0) in enumerate(gspecs):
                      wk = wk_next
                      gslot = gslots[gi_]
                      if gi_ + 1 < 3:
                          c0n, ntn, _ = gspecs[gi_ + 1]
                          wk_next = load_w(gslots[gi_ + 1], [(w_in[l][:, c0n:c0n + ntn * 128], 0)], l)
                      else:
                          wk_next = load_w(0, [(w_in[l][:, 0:384], 0)], l)
                      for c in range(NCH):
                          ht, hk = load_hc(c)
                          for j in range(ntile):
                              p, pk = proj(ht, hk, gslot, wk, j * 128, 128)
                              og, ok = ogrot.next()
                              op('act', 'activation', r=[pk], w=[ok], out=og[:], in_=p[:], func=AF.Silu)
                              dma('pool', gate_d[c, :, row0 + j, :], og[:], r=[ok],
                                  w=[('gated', row0 + j, c)], sem=ok)

                  chk(3)
                  def latent_norm(ntile, gcol0, width, dst_d, dkey, wk, slot, c, ht, hk, lt, lk):
                      tk = []
                      for j in range(ntile):
                          p, pk = proj(ht, hk, slot, wk, j * 128, 128)
                          KD0 = int(_os.environ.get('KDBG', '9'))
                          if KD0 < 0:
                              continue
                          t1, k1 = tmprot.next()
                          op('dve', 'tensor_copy', r=[pk], w=[k1], out=t1[:], in_=p[:])
                          tk.append((t1, k1))
                          if KD0 < 1:
                              continue
                          pt, ptk = ptrot.next()
                          op('dve', 'tensor_tensor', r=[k1], w=[ptk], out=pt[:], in0=t1[:], in1=t1[:], op=ALU.mult)
                          mm(ssp[:, :], ones_b[:], pt[:], j == 0, j == ntile - 1, r=['ones_b', ptk], w=[('sc', 2)])
                      KD = int(_os.environ.get('KDBG', '9'))
                      if KD < 2:
                          return
                      t2, k2 = tmprot.next()
                      op('act', 'activation', r=[('sc', 2)], w=[k2], out=t2[:], in_=ssp[:], func=AF.Ln, scale=1.0 / width, bias=EPS)
                      if KD < 3:
                          return
                      t3, k3 = tmprot.next()
                      op('act', 'activation', r=[k2], w=[k3], out=t3[:], in_=t2[:], func=AF.Exp, scale=-0.5)
                      if KD < 4:
                          return
                      for j in range(ntile):
                          t1, k1 = tk[j]
                          op('dve', 'scalar_tensor_tensor', r=[k1, k3, 'c_gq', 'c_gkv'], w=[lk], out=lt[:, j, :], in0=t1[:],
                             scalar=cols[:, gcol0 + j:gcol0 + j + 1], in1=t3[:], op0=ALU.mult, op1=ALU.mult)
                      dma('pool', dst_d[c], lt[:, 0:ntile, :], r=[lk],
                          w=[(dkey, c)], sem=lk)

                  wk = wk_next
                  wk_ckv = load_w(1, [(w_in[l][:, 384:672], 0), (w_krp[l], 288)], l)
                  for c in range(NCH):
                      ht, hk = load_hc(c)
                      lt, lk = latcrot.next()
                      latent_norm(3, 32, 384.0, cqn_d, 'cqnd', wk, 0, c, ht, hk, lt, lk)
                  wk = wk_ckv
                  for c in range(NCH):
                      ht, hk = load_hc(c)
                      lt, lk = latkrot.next()
                      latent_norm(2, 36, 256.0, ckvn_d, 'ckvnd', wk, 1, c, ht, hk, lt, lk)
                      rt, rk = r32rot.next()
                      S.add('sp', (lambda e, rt=rt, c=c: [e.dma_start(out=rt[0:32, 0, :], in_=cos32_d[:, c * 512:(c + 1) * 512]),
                                                          e.dma_start(out=rt[0:32, 1, :], in_=ssin32_d[:, c * 512:(c + 1) * 512])]),
                            r=[], w=[rk], dma=2, semkey=rk)
                      pa, pak = proj(ht, hk, 1, wk, 256, 32)
                      pb, pbk = proj(ht, hk, 1, wk, 288, 32)
                      t1, k1 = tmprot.next()
                      op('dve', 'tensor_tensor', r=[pak, rk], w=[k1], out=t1[0:32, :], in0=pa[0:32, :], in1=rt[0:32, 0, :], op=ALU.mult)
                      t2, k2 = tmprot.next()
                      op('dve', 'tensor_tensor', r=[pbk, rk], w=[k2], out=t2[0:32, :], in0=pb[0:32, :], in1=rt[0:32, 1, :], op=ALU.mult)
                      kt_, kk = krcrot.next()
                      op('dve', 'tensor_tensor', r=[k1, k2], w=[kk], out=kt_[:], in0=t1[0:32, :], in1=t2[0:32, :], op=ALU.add)
                      dma('pool', kr_d[:, c * 512:(c + 1) * 512], kt_[:], r=[kk], w=[('krd', c)], sem=kk)

                  prefetch = {}
                  deferred = []
                  prefetch[('dil', 0, 0)] = load_w(0, [(w_in[l][:, C_QKVB:C_QKVB + 128], 0),
                                                       (w_in[l][:, C_QKVB + 768:C_QKVB + 768 + 128], 128),
                                                       (w_in[l][:, C_QKVB + 1536:C_QKVB + 1536 + 128], 256)], l)
                  chk(4)
                  for u in range(4):
                      for c in range(NCH):
                          lt, lk = latkrot.next()
                          dma('sp', lt[:, 0:2, :], ckvn_d[c], r=[('ckvnd', c)], w=[lk], sem=lk)
                          kt_, kk = krcrot.next()
                          dma('sp', kt_[:], kr_d[:, c * 512:(c + 1) * 512], r=[('krd', c)], w=[kk], sem=kk)
                          for e in range(2):
                              h = 2 * u + e
                              p, pk = pjrot.next()
                              for kc in range(2):
                                  mm(p[0:64, :], wukv_s[:, kc, h * 128:h * 128 + 64], lt[:, kc, :], kc == 0, kc == 1, r=['wmla', lk], w=[pk])
                              op('act', 'activation', r=[pk], w=[('KQ', e)], out=KQ[e][0:64, c * 512:(c + 1) * 512], in_=p[0:64, :], func=AF.Copy)
                              op('dve', 'tensor_copy', r=[kk], w=[('KQ', e)], out=KQ[e][64:96, c * 512:(c + 1) * 512], in_=kt_[:])
                              p, pk = pjrot.next()
                              for kc in range(2):
                                  mm(p[0:64, :], wukv_s[:, kc, h * 128 + 64:h * 128 + 128], lt[:, kc, :], kc == 0, kc == 1, r=['wmla', lk], w=[pk])
                              op('act', 'activation', r=[pk], w=[('VTc', c)], out=VT[e * 64:(e + 1) * 64, c * 512:(c + 1) * 512], in_=p[0:64, :], func=AF.Copy)
                          if c > 0:
                              v_transposes_c(c - 1)
                      v_transposes_c(NCH - 1)
                      qres = {}

                      def q_loads(c):
                          lt, lk = latcrot.next()
                          dma('sp', lt[:], cqn_d[c], r=[('cqnd', c)], w=[lk], sem=lk)
                          rt, rk = r32rot.next()
                          S.add('sp', (lambda e_, rt=rt, c=c: [e_.dma_start(out=rt[64:96, 0, :], in_=cos32_d[:, c * 512:(c + 1) * 512]),
                                                               e_.dma_start(out=rt[64:96, 1, :], in_=ssin32_d[:, c * 512:(c + 1) * 512])]),
                                r=[], w=[rk], dma=2, semkey=rk)
                          qres[c] = (lt, lk, rt, rk)

                      def q_prep(c, e):
                          (lt, lk, rt, rk) = qres[c]
                          h = 2 * u + e
                          pa, pak = pjrot.next()
                          for kc in range(3):
                              mm(pa[0:96, :], wuqh_s[:, kc, h * 96:(h + 1) * 96], lt[:, kc, :], kc == 0, kc == 2, r=['wmla', lk], w=[pak])
                          pb, pbk = pjrot.next()
                          for kc in range(3):
                              mm(pb[0:96, :], wuqrp_s[:, kc, h * 96:(h + 1) * 96], lt[:, kc, :], kc == 0, kc == 2, r=['wmla', lk], w=[pbk])
                          qt, qk = qcrot.next()
                          t1, k1 = tmprot.next()
                          op('dve', 'tensor_tensor', r=[pak, rk], w=[k1], out=t1[64:96, :], in0=pa[64:96, :], in1=rt[64:96, 0, :], op=ALU.mult)
                          t2, k2 = tmprot.next()
                          op('dve', 'tensor_tensor', r=[pbk, rk], w=[k2], out=t2[64:96, :], in0=pb[64:96, :], in1=rt[64:96, 1, :], op=ALU.mult)
                          op('dve', 'tensor_tensor', r=[k1, k2], w=[qk], out=qt[64:96, :], in0=t1[64:96, :], in1=t2[64:96, :], op=ALU.add)
                          op('dve', 'tensor_copy', r=[pak], w=[qk], out=qt[0:64, :], in_=pa[0:64, :])
                          return qt, qk

                      jobs = [(c, e) for c in range(NCH) for e in range(2)]
                      q_loads(0)
                      if NCH > 1:
                          q_loads(1)
                      qcur = q_prep(0, 0)
                      gres = None
                      for ji, (c, e) in enumerate(jobs):
                          qnext = None
                          if ji + 1 < len(jobs):
                              c2, e2 = jobs[ji + 1]
                              if e2 == 0 and c2 + 1 < NCH:
                                  q_loads(c2 + 1)
                              qnext = q_prep(c2, e2)
                          if e == 0:
                              gt, gk = load_gate(u, c)
                              og, ok = ogrot.next()
                              gres = (gt, gk, og, ok)
                          (gt, gk, og, ok) = gres
                          qt, qk = qcur
                          blocks = [(KQ[e][0:96, j * 128:(j + 1) * 128], qt[0:96, :], 0, 512, None, Var[:, j, e * 64:e * 64 + 128],
                                     [('KQ', e), qk]) for j in range(NT)]
                          acc, ak = attend(blocks, 96.0 ** -0.5)
                          finish_head(e, acc, acc, [ak], gt, gk, og, ok)
                          if e == 1:
                              store_og(og, ok, u, c)
                          qcur = qnext


                  def na_segs(hh_):
                      return [(w_in[l][:, C_QKVC + hh_ * 64:C_QKVC + hh_ * 64 + 128], 0),
                              (w_in[l][:, C_QKVC + 512 + hh_ * 64:C_QKVC + 512 + hh_ * 64 + 128], 128),
                              (w_in[l][:, C_QKVC + 1024 + hh_ * 64:C_QKVC + 1024 + hh_ * 64 + 128], 256)]
                  for pp in range(2):
                      for gi, dd in enumerate((1, 4, 16)):
                          L = SL // dd
                          hh = gi * 4 + 2 * pp
                          def dil_segs(hh_):
                              return [(w_in[l][:, C_QKVB + hh_ * 64:C_QKVB + hh_ * 64 + 128], 0),
                                      (w_in[l][:, C_QKVB + 768 + hh_ * 64:C_QKVB + 768 + hh_ * 64 + 128], 128),
                                      (w_in[l][:, C_QKVB + 1536 + hh_ * 64:C_QKVB + 1536 + hh_ * 64 + 128], 256)]
                          wk = prefetch.pop(('dil', pp, gi), None)
                          if wk is None:
                              wk = load_w(gi % 2, dil_segs(hh), l)

                          def cm(buf, c, dd=dd, L=L):
                              if dd == 1:
                                  return buf[:, c * 512:(c + 1) * 512]
                              m0 = c * 512 // dd
                              return buf[:, 0:SL].rearrange("p (r m) -> p m r", r=dd)[:, m0:m0 + 512 // dd, :]

                          def nat(t):
                              return t[:] if dd == 1 else t[:].rearrange("p (m r) -> p m r", r=dd)

                          for c in range(NCH):
                              ht, hk = load_hc(c)
                              rt, rk = r64rot.next()
                              S.add('sp', (lambda e_, rt=rt, c=c: [e_.dma_start(out=rt[:, 0, :], in_=cos64_d[:, c * 512:(c + 1) * 512]),
                                                                   e_.dma_start(out=rt[:, 1, :], in_=ssin64_d[:, c * 512:(c + 1) * 512])]),
                                    r=[], w=[rk], dma=2, semkey=rk)
                              for which, dstbuf, dkey in ((0, KQ[1], ('KQ', 1)), (1, KQ[0], ('KQ', 0))):
                                  p, pk = proj(ht, hk, gi % 2, wk, which * 128, 128)
                                  t1, k1 = tmprot.next()
                                  op('dve', 'tensor_tensor', r=[pk, rk], w=[k1], out=t1[:], in0=p[:], in1=rt[:, 0, :], op=ALU.mult)
                                  t2, k2 = tmprot.next()
                                  for blk in range(4):
                                      srcb = blk ^ 1
                                      op('dve', 'tensor_tensor', r=[pk, rk], w=[k2], out=t2[blk * 32:(blk + 1) * 32, :],
                                         in0=p[srcb * 32:(srcb + 1) * 32, :], in1=rt[blk * 32:(blk + 1) * 32, 1, :], op=ALU.mult)
                                  op('pool', 'tensor_tensor', r=[k1, k2], w=[dkey], out=cm(dstbuf, c), in0=nat(t1), in1=nat(t2), op=ALU.add)
                              p, pk = proj(ht, hk, gi % 2, wk, 256, 128)
                              op('act', 'activation', r=[pk], w=['VT'], out=cm(VT, c), in_=(p[:] if dd == 1 else p[:].rearrange("p (m r) -> p m r", r=dd)),
                                 func=AF.Copy)
                          if gi < 2:
                              prefetch[('dil', pp, gi + 1)] = load_w((gi + 1) % 2, dil_segs((gi + 1) * 4 + 2 * pp), l)
                          while deferred:
                              deferred.pop(0)()
                          v_transposes(NT)
                          for c in range(NCH):
                              segs = []
                              lo, hi = c * 512, (c + 1) * 512
                              for rcls in range(lo // L, (hi - 1) // L + 1):
                                  s0, s1 = max(lo, rcls * L), min(hi, (rcls + 1) * L)
                                  segs.append((rcls, s0 - lo, s1 - s0, s0 - rcls * L))
                              for e in range(2):
                                  blocks = []
                                  for (rcls, a, wdt, mq0) in segs:
                                      for j in range(rcls * L // 128, (rcls + 1) * L // 128):
                                          mk0 = j * 128 - rcls * L
                                          if mk0 + 127 < mq0 - 64 or mk0 > mq0 + wdt - 1 + 64:
                                              continue
                                          dl = mk0 - mq0
                                          blocks.append((KQ[0][e * 64:(e + 1) * 64, j * 128:(j + 1) * 128],
                                                         KQ[1][e * 64:(e + 1) * 64, lo + a:lo + a + wdt], a, wdt,
                                                         tm[:, 512 - dl:512 - dl + wdt], Var[:, j, e * 64:e * 64 + 128], [('KQ', 0), ('KQ', 1)]))
                                  acc, ak = attend(blocks, 0.125)
                                  nr = slice(e * 64, e * 64 + 64)
                                  dr = slice(64 - e * 64, 128 - e * 64)
                                  for (rcls, a, wdt, mq0) in segs:
                                      n0 = rcls + dd * mq0
                                      for (accbuf, rows, key) in ((numacc, nr, 'numacc'), (denacc, dr, 'denacc')):
                                          oap = accbuf[rows, n0:n0 + dd * (wdt - 1) + 1:dd]
                                          if gi == 0:
                                              op('dve', 'tensor_copy', r=[ak], w=[key], out=oap, in_=acc[rows, a:a + wdt])
                                          else:
                                              op('dve', 'tensor_tensor', r=[ak, key], w=[key], out=oap, in0=acc[rows, a:a + wdt], in1=oap, op=ALU.add)
                      if pp == 0:
                          prefetch[('dil', 1, 0)] = load_w(0, dil_segs(2), l)
                      else:
                          prefetch[('na', 0)] = load_w(0, na_segs(0), l)
                      def dil_normalize(pp=pp):
                          for c in range(NCH):
                              gt, gk = load_gate(4 + pp, c)
                              og, ok = ogrot.next()
                              for e in range(2):
                                  finish_head(e, numacc[:, c * 512:(c + 1) * 512], denacc[:, c * 512:(c + 1) * 512], ['numacc', 'denacc'], gt, gk, og, ok, split=True)
                              store_og(og, ok, 4 + pp, c)
                      deferred.append(dil_normalize)

                  chk(6)
                  rows = SL // 64
                  for pp in range(4):
                      hh = 2 * pp
                      wk = prefetch.pop(('na', pp), None)
                      if wk is None:
                          wk = load_w(pp % 2, na_segs(hh), l)
                      def etab_step(e, var):
                          h = hh + e
                          base = rpbf[l, h]
                          if var == 0:
                              S.add('sp', (lambda e_, base=base: [e_.dma_start(
                                  out=Xraw[kr * 64:(kr + 1) * 64, :].rearrange("p (s q) -> p s q", q=64),
                                  in_=bass.AP(tensor=base.tensor, offset=base.offset + kr * 127, ap=[[1, 64], [127, 23], [1, 64]])) for kr in range(2)]),
                                    r=[], w=['ogc'], dma=2, semkey='ogc')
                          op('dve', 'scalar_tensor_tensor', r=['ogc', 'cmask'], w=['Xv'], out=Xv[:], in0=Xraw[:], scalar=8.0, in1=cmask[:, var, :],
                             op0=ALU.mult, op1=ALU.add)
                          for q0 in range(0, 23 * 64, 512):
                              q1 = min(q0 + 512, 23 * 64)
                              p, pk = pj4rot.next()
                              mm(p[:, 0:q1 - q0], jmat[:], Xv[:, q0:q1], True, True, r=['jmat', 'Xv'], w=[pk])
                              op('act', 'activation', r=[pk], w=['Etab'], out=Etab[:, e * 2 + var, q0:q1], in_=p[:, 0:q1 - q0], func=AF.Copy)

                      esteps = [(0, 0), (0, 1), (1, 0), (1, 1)]
                      for c in range(NCH):
                          ht, hk = load_hc(c)
                          for which, dstbuf, dkey in ((0, KQ[1], ('KQ', 1)), (1, KQ[0], ('KQ', 0)), (2, VT, ('VTc', c))):
                              p, pk = proj(ht, hk, pp % 2, wk, which * 128, 128)
                              if which == 1:
                                  op('dve', 'tensor_copy', r=[pk], w=[dkey], out=dstbuf[:, c * 512:(c + 1) * 512], in_=p[:])
                              else:
                                  op('act', 'activation', r=[pk], w=[dkey], out=dstbuf[:, c * 512:(c + 1) * 512], in_=p[:], func=AF.Copy)
                          if c > 0:
                              v_transposes_c(c - 1)
                          if esteps:
                              etab_step(*esteps.pop(0))
                      while esteps:
                          etab_step(*esteps.pop(0))
                      if pp < 3:
                          prefetch[('na', pp + 1)] = load_w((pp + 1) % 2, na_segs(2 * (pp + 1)), l)
                      else:
                          prefetch[('mix', 0, 0)] = load_w(0, [(w_in[l][:, C_MG + b * 1024:C_MG + b * 1024 + 128], b * 128) for b in range(3)], l)
                          S.add('pool', (lambda e_, l=l: [
                              e_.dma_start(out=wp_s[:, 0:4, :], in_=kt_view(w_pa[l][:, 0:128])),
                              e_.dma_start(out=wp_s[:, 4:6, :], in_=kt_view(w_pb[l][:, 0:128])),
                              e_.dma_start(out=wp_s[:, 6:10, :], in_=kt_view(w_pc[l][:, 0:128]))]),
                                r=[], w=['wp'], dma=3, semkey='wp')
                      while deferred:
                          deferred.pop(0)()
                      v_transposes_c(NCH - 1)
                      for c in range(NCH):
                          R0 = 8 * c
                          if c == 0:
                              segs = [(0, 4, 1, list(range(0, 4))), (4, 4, 0, None)]
                          elif c == NCH - 1:
                              segs = [(R0, 5, 0, None), (R0 + 5, 3, 1, list(range(R0 // 2, R0 // 2 + 4)))]
                          else:
                              segs = [(R0, 8, 0, None)]
                          gt, gk = load_gate(6 + pp, c)
                          og, ok = ogrot.next()
                          for e in range(2):
                              blocks = []
                              for (qa, nq, var, jl) in segs:
                                  if jl is None:
                                      jl = list(range(max(0, qa - 4) // 2, min(rows - 1, qa + nq + 2) // 2 + 1))
                                  a = (qa - R0) * 64
                                  for j in jl:
                                      s0 = qa - 2 * j + 10
                                      assert 0 <= s0 and s0 + nq <= 23
                                      blocks.append((KQ[0][e * 64:(e + 1) * 64, j * 128:(j + 1) * 128],
                                                     KQ[1][e * 64:(e + 1) * 64, c * 512 + a:c * 512 + a + nq * 64], a, nq * 64,
                                                     Etab[:, e * 2 + (1 - var), s0 * 64:(s0 + nq) * 64], Var[:, j, e * 64:e * 64 + 128],
                                                     [('KQ', 0), ('KQ', 1)]))
                              acc, ak = attend(blocks, 0.125)
                              finish_head(e, acc, acc, [ak], gt, gk, og, ok)
                          store_og(og, ok, 6 + pp, c)

                  chk(7)
                  wp2 = Xv[:, 0:1280].rearrange("p (k n) -> p k n", n=128)
                  for fp in range(4):
                      fs = (2 * fp, 2 * fp + 1)
                      def mg_segs(f_):
                          return [(w_in[l][:, C_MG + b * 1024 + f_ * 128:C_MG + b * 1024 + (f_ + 1) * 128], b * 128) for b in range(3)]

                      def wp_load(f_, wt, key):
                          S.add('pool', (lambda e_, l=l, f=f_, wt=wt: [
                              e_.dma_start(out=wt[:, 0:4, :], in_=kt_view(w_pa[l][:, f * 128:(f + 1) * 128])),
                              e_.dma_start(out=wt[:, 4:6, :], in_=kt_view(w_pb[l][:, f * 128:(f + 1) * 128])),
                              e_.dma_start(out=wt[:, 6:10, :], in_=kt_view(w_pc[l][:, f * 128:(f + 1) * 128]))]),
                                r=[], w=[key], dma=3, semkey=key)
                      wps = (wp_s, wp2)
                      wpk = ('wp', 'Xv')
                      wks = []
                      for j, f in enumerate(fs):
                          pre = prefetch.pop(('mix', fp, j), None)
                          if pre is None:
                              wks.append(load_w(j, mg_segs(f), l))
                              wp_load(f, wps[j], wpk[j])
                          else:
                              wks.append(pre)
                      for j, f in enumerate(()):
                          S.add('pool', (lambda e_, l=l, f=f, wt=wps[j]: [
                              e_.dma_start(out=wt[:, 0:4, :], in_=kt_view(w_pa[l][:, f * 128:(f + 1) * 128])),
                              e_.dma_start(out=wt[:, 4:6, :], in_=kt_view(w_pb[l][:, f * 128:(f + 1) * 128])),
                              e_.dma_start(out=wt[:, 6:10, :], in_=kt_view(w_pc[l][:, f * 128:(f + 1) * 128]))]),
                                r=[], w=[wpk[j]], dma=3, semkey=wpk[j])
                      for c in range(NCH):
                          ht, hk = load_hc(c)
                          dma('sp', ogc[:], og_d[c],
                              r=[('ogd', k, c) for k in range(10)], w=['ogc'], sem='ogc')
                          for j, f in enumerate(fs):
                              mx, mxk = tmprot.next()
                              og, ok = ogrot.next()
                              sgs = {}
                              if j == 0:
                                  for b in range(3):
                                      p, pk = proj(ht, hk, j, wks[j], b * 128, 128)
                                      sg, sgk = tmprot.next()
                                      op('act', 'activation', r=[pk], w=[sgk], out=sg[:], in_=p[:], func=AF.Sigmoid)
                                      sgs[b] = (sg, sgk)
                              for b, (k0, k1) in enumerate(((0, 4), (4, 6), (6, 10))):
                                  if j != 0:
                                      p, pk = proj(ht, hk, j, wks[j], b * 128, 128)
                                      sg, sgk = tmprot.next()
                                      op('act', 'activation', r=[pk], w=[sgk], out=sg[:], in_=p[:], func=AF.Sigmoid)
                                  else:
                                      sg, sgk = sgs[b]
                                  p2, pk2 = pj4rot.next()
                                  for kt in range(k0, k1):
                                      mm(p2[:, :], wps[j][:, kt, :], ogc[:, kt, :], kt == k0, kt == k1 - 1, r=[wpk[j], 'ogc'], w=[pk2])
                                  if b == 0:
                                      op('dve', 'tensor_tensor', r=[pk2, sgk], w=[mxk], out=mx[:], in0=p2[:], in1=sg[:], op=ALU.mult)
                                  else:
                                      tq, tqk = tmprot.next()
                                      op('dve', 'tensor_tensor', r=[pk2, sgk], w=[tqk], out=tq[:], in0=p2[:], in1=sg[:], op=ALU.mult)
                                      if b == 1:
                                          op('dve', 'tensor_tensor', r=[mxk, tqk], w=[mxk], out=mx[:], in0=mx[:], in1=tq[:], op=ALU.add)
                                      else:
                                          op('dve', 'tensor_tensor', r=[mxk, tqk], w=[ok], out=og[:], in0=mx[:], in1=tq[:], op=ALU.add)
                              dma('pool', mix_d[c, :, f, :], og[:], r=[ok], w=[('mixd', f, c)], sem=ok)
                              if c == NCH - 1 and j == 0:
                                  if fp < 3:
                                      prefetch[('mix', fp + 1, 0)] = load_w(0, mg_segs(2 * (fp + 1)), l)
                                      wp_load(2 * (fp + 1), wp_s, 'wp')
                                  else:
                                      prefetch[('out', 0)] = load_w(0, [(w_out[l][:, 0:512], 0)], l)

                  wk0 = prefetch.pop(('out', 0), None)
                  if wk0 is None:
                      wk0 = load_w(0, [(w_out[l][:, 0:512], 0)], l)
                  wk1 = load_w(1, [(w_out[l][:, 512:1024], 0)], l)
                  for c in range(NCH):
                      mc, mk = hcrot.next()
                      dma('sp', mc[:], mix_d[c],
                          r=[('mixd', f, c) for f in range(8)], w=[mk], sem=mk)
                      for i in range(4):
                          t = 4 * c + i
                          xtt, xk = xtrot.next()
                          dma('sp', xtt[:], src_x[t * 128:(t + 1) * 128, :], r=[('y', si, t)], w=[xk], sem=xk)
                          xnt, xnk = xnrot.next()
                          for half, wkk in ((0, wk0), (1, wk1)):
                              for kt in range(8):
                                  mm(pj[half][:, :], mc[:, kt, i * 128:(i + 1) * 128], wA[half][:, kt, :], kt == 0, kt == 7, r=[mk, wkk], w=[('pj', half)])
                              op('act', 'activation', r=[('pj', half)], w=[xnk], out=xnt[:, half * 512:(half + 1) * 512], in_=pj[half][:, :], func=AF.Copy)
                              op('dve', 'scalar_tensor_tensor', r=[xnk], w=[xnk, ('ssh', half)], out=xnt[:, half * 512:(half + 1) * 512],
                                 in0=xnt[:, half * 512:(half + 1) * 512], scalar=1.0, in1=xnt[:, half * 512:(half + 1) * 512],
                                 op0=ALU.mult, op1=ALU.mult, accum_out=small[:, 4 + half:5 + half])
                          op('dve', 'tensor_tensor', r=[('ssh', 0), ('ssh', 1)], w=['ssq2'], out=small[:, 6:7], in0=small[:, 4:5], in1=small[:, 5:6], op=ALU.add)
                          op('act', 'activation', r=['ssq2'], w=['sd2'], out=small[:, 7:8], in_=small[:, 6:7], func=AF.Sqrt, scale=1.0 / D, bias=EPS)
                          op('dve', 'reciprocal', r=['sd2'], w=['rs2'], out=small[:, 8:9], in_=small[:, 7:8])
                          for half in range(2):
                              op('dve', 'scalar_tensor_tensor', r=[('pj', half), 'rs2', 'G_bc'], w=[xnk], out=xnt[:, half * 512:(half + 1) * 512],
                                 in0=pj[half][:, :], scalar=small[:, 8:9], in1=G_bc[:, half * 512:(half + 1) * 512], op0=ALU.mult, op1=ALU.mult)
                          op('pool', 'tensor_tensor', r=[xnk, xk], w=[xk], out=xtt[:], in0=xnt[:], in1=xtt[:], op=ALU.add)
                          dma('pool', ydst[t * 128:(t + 1) * 128, :], xtt[:], r=[xk], w=[('y', si, t)], sem=xk)

        except _Stop:
            pass
        S.add('sp', (lambda e: None), r=[('y', si, t) for si, SL in enumerate(seq_lens) for t in range(SL // 128)])
        S.add('pool', (lambda e: None), r=[('y', si, t) for si, SL in enumerate(seq_lens) for t in range(SL // 128)])
        run_stream, nsem = S.emit(nc, lambda name: es.enter_context(nc.semaphore(name)))
        with nc.Block() as block:
            @block.sync
            def _(e):
                run_stream('sp', e)

            @block.scalar
            def _(e):
                run_stream('act', e)

            @block.vector
            def _(e):
                run_stream('dve', e)

            @block.gpsimd
            def _(e):
                run_stream('pool', e)

            @block.tensor
            def _(e):
                run_stream('pe', e)
    build_program.stats = STATS
    return nc, len(S.ops)


def host_consts(Smax):
    bf = ml_dtypes.bfloat16
    out = {}
    out["ident_f"] = np.eye(128, dtype=np.float32)
    out["ident_b"] = np.eye(128, dtype=np.float32).astype(bf)
    out["jmat"] = np.eye(128, dtype=np.float32)[::-1].copy().astype(bf)
    out["ones_b"] = np.ones((128, 128), dtype=np.float32).astype(bf)
    pos = np.arange(Smax, dtype=np.float32)
    inv64 = (np.float32(10000.0) ** (-np.arange(32, dtype=np.float32) * np.float32(2.0) / np.float32(64))).astype(np.float32)
    ang = (pos[None, :] * inv64[:, None]).astype(np.float32)
    c64, s64 = np.cos(ang).astype(np.float32), np.sin(ang).astype(np.float32)
    cos64 = np.zeros((128, Smax), np.float32); ssin64 = np.zeros((128, Smax), np.float32)
    for p in range(128):
        cos64[p] = c64[p % 32]
        ssin64[p] = -s64[p % 32] if (p % 64) < 32 else s64[p % 32]
    out["cos64"], out["ssin64"] = cos64, ssin64
    inv32 = (np.float32(10000.0) ** (-np.arange(16, dtype=np.float32) * np.float32(2.0) / np.float32(32))).astype(np.float32)
    ang = (pos[None, :] * inv32[:, None]).astype(np.float32)
    c32, s32 = np.cos(ang).astype(np.float32), np.sin(ang).astype(np.float32)
    cos32 = np.zeros((32, Smax), np.float32); ssin32 = np.zeros((32, Smax), np.float32)
    for p in range(32):
        cos32[p] = c32[p % 16]
        ssin32[p] = -s32[p % 16] if p < 16 else s32[p % 16]
    out["cos32"], out["ssin32"] = cos32, ssin32
    kp = np.arange(128)[:, None]
    u = np.arange(1152)[None, :] - 512
    out["tm"] = np.where(np.abs(kp - u) <= 64, 0.0, NEG).astype(np.float32).astype(bf)
    cm = np.zeros((128, 2, 23, 64), np.float32)
    for pr in range(128):
        krp, kcp = pr // 64, pr % 64
        kcol = 63 - kcp
        for sp_ in range(23):
            roff = 18 - (sp_ + krp)
            for var, (lo, hi) in enumerate(((0, 14), (3, 10))):
                rv = lo <= roff <= hi
                qcol = np.arange(64)
                cs = np.clip(qcol - 8, 0, 48)
                cv = (kcol >= cs) & (kcol < cs + 16)
                cm[pr, var, sp_, :] = np.where(cv & rv, 0.0, NEG)
    out["cmask"] = cm.reshape(128, 2, 23 * 64).astype(bf)
    return out


def host_weights(w):
    out = {}
    depth = w["w_in"].shape[0]
    for k in ("w_ada", "b_ada", "g_pre", "g_post", "w_in", "g_q", "g_kv", "w_ukv", "w_pa", "w_pb", "w_pc", "w_out"):
        out[k] = np.ascontiguousarray(w[k], dtype=np.float32)
    perm = np.concatenate([np.arange(16, 32), np.arange(0, 16)])
    out["w_krp"] = np.ascontiguousarray(w["w_in"][:, :, 640 + perm])
    uq = np.asarray(w["w_uq"]).reshape(depth, 384, 8, 96)
    out["w_uqh"] = np.ascontiguousarray(uq.reshape(depth, 384, 768))
    out["w_uqrp"] = np.ascontiguousarray(np.concatenate([uq[..., 0:64], uq[..., 64 + perm]], axis=-1).reshape(depth, 384, 768))
    rp = np.asarray(w["rpb"])
    rf = np.zeros((depth, 8, 24, 127), np.float32)
    for u in range(4, 19):
        rf[:, :, u, 48:79] = rp[:, :, 18 - u, ::-1]
    out["rpbf"] = rf
    return out


_PROG_CACHE = {}


def run_cores(seq_lens, depth, per_core_inputs):
    key = (tuple(seq_lens), depth)
    if key not in _PROG_CACHE:
        _PROG_CACHE[key] = build_program(list(seq_lens), depth)[0]
    nc = _PROG_CACHE[key]
    res = run_bass_kernel_spmd(nc, per_core_inputs, core_ids=list(range(len(per_core_inputs))))
    return res.results


def kernel(x_prompt, x_sample, c_prompt, c_sample, w_ada, b_ada, g_pre, g_post, w_in, g_q, w_uq, g_kv, w_ukv, rpb,
           w_pa, w_pb, w_pc, w_out):
    ncore = 8
    x_prompt = np.asarray(x_prompt, dtype=np.float32); x_sample = np.asarray(x_sample, dtype=np.float32)
    c_prompt = np.asarray(c_prompt, dtype=np.float32); c_sample = np.asarray(c_sample, dtype=np.float32)
    depth = np.asarray(w_in).shape[0]
    W = host_weights(dict(w_ada=np.asarray(w_ada), b_ada=np.asarray(b_ada), g_pre=np.asarray(g_pre), g_post=np.asarray(g_post),
                          w_in=np.asarray(w_in), g_q=np.asarray(g_q), w_uq=np.asarray(w_uq), g_kv=np.asarray(g_kv),
                          w_ukv=np.asarray(w_ukv), rpb=np.asarray(rpb), w_pa=np.asarray(w_pa), w_pb=np.asarray(w_pb),
                          w_pc=np.asarray(w_pc), w_out=np.asarray(w_out)))
    seq_lens = [x_sample.shape[1], x_prompt.shape[1], x_prompt.shape[1]]
    C = host_consts(max(seq_lens))
    in_maps = []
    for i in range(ncore):
        m = dict(W)
        m.update(C)
        m["x0"] = np.ascontiguousarray(x_sample[i]); m["x1"] = np.ascontiguousarray(x_prompt[2 * i]); m["x2"] = np.ascontiguousarray(x_prompt[2 * i + 1])
        m["c"] = np.ascontiguousarray(np.stack([c_sample[i], c_prompt[2 * i], c_prompt[2 * i + 1]]))
        in_maps.append(m)
    res = run_cores(seq_lens, depth, in_maps)
    y_prompt = np.empty_like(x_prompt); y_sample = np.empty_like(x_sample)
    for i in range(ncore):
        y_sample[i] = res[i]["y0"]; y_prompt[2 * i] = res[i]["y1"]; y_prompt[2 * i + 1] = res[i]["y2"]
    return (y_prompt, y_sample)
```
